# Optimizing a Trainium2 kernel written in Bass

```python
import jax, jax.numpy as jnp
from jax import lax
import numpy as np

D_MODEL = 2048
BATCH = 8
SEQ = 2048
DEPTH = 4

N_MIXERS = 4
HEAD_DIM = 128
N_HEADS = D_MODEL // HEAD_DIM
Q_BLOCK = 128
EPS = 1e-6

FOX_FGATE_BIAS = 2.0

MLA_Q_RANK = 512
MLA_KV_RANK = 512
MLA_NOPE = 128
MLA_ROPE = 64
MLA_V = 128
ROPE_THETA = 10000.0

SGU_CHUNK = 128
SGU_WIDTH = D_MODEL
SGU_GROUP_DIM = 128
SGU_GROUPS = SGU_WIDTH // SGU_GROUP_DIM

D_FF = 5632
CONV_WIDTH = 3

kernel_name = 'hybrid_interleaved_fox_mla_stickbreak_sgu'


def _n_layers_of(m):
    return len(range(m, DEPTH, N_MIXERS))


def rms_norm(x, gain):
    x32 = x.astype(jnp.float32)
    y = x32 * lax.rsqrt(jnp.mean(x32 * x32, axis=-1, keepdims=True) + EPS)
    return (y * gain.astype(jnp.float32)).astype(x.dtype)


def _sweep_query_blocks(block_fn, seq):
    out = lax.map(block_fn, jnp.arange(seq // Q_BLOCK))
    nb, b, qb, h, dv = out.shape
    return out.transpose(1, 0, 2, 3, 4).reshape(b, seq, h * dv)


def _causal_softmax(s, start, v):
    seq = s.shape[-1]
    q_pos = start + jnp.arange(Q_BLOCK)
    allowed = jnp.arange(seq)[None, :] <= q_pos[:, None]
    p = jax.nn.softmax(jnp.where(allowed, s, -jnp.inf), axis=-1)
    return jnp.einsum('bhqs,bshd->bqhd', p.astype(v.dtype), v)


def fox_mixer(h, w_in, b_f, q_gain, k_gain, w_out):
    bsz, seq, _ = h.shape
    hd = N_HEADS * HEAD_DIM
    q, k, v, f_logit = jnp.split(h @ w_in, [hd, 2 * hd, 3 * hd], axis=-1)
    q = rms_norm(q.reshape(bsz, seq, N_HEADS, HEAD_DIM), q_gain)
    k = rms_norm(k.reshape(bsz, seq, N_HEADS, HEAD_DIM), k_gain)
    v = v.reshape(bsz, seq, N_HEADS, HEAD_DIM)
    log_f = jax.nn.log_sigmoid(f_logit.astype(jnp.float32) + b_f.astype(jnp.float32))
    cum = jnp.cumsum(log_f, axis=1).transpose(0, 2, 1)
    scale = HEAD_DIM ** -0.5

    def block(i):
        start = i * Q_BLOCK
        qb = lax.dynamic_slice_in_dim(q, start, Q_BLOCK, axis=1)
        cq = lax.dynamic_slice_in_dim(cum, start, Q_BLOCK, axis=2)
        s = jnp.einsum('bqhd,bshd->bhqs', qb, k, preferred_element_type=jnp.float32) * scale
        s = s + (cq[..., :, None] - cum[:, :, None, :])
        return _causal_softmax(s, start, v)

    return _sweep_query_blocks(block, seq) @ w_out


def _rope_tables(positions):
    inv_freq = ROPE_THETA ** (-jnp.arange(0, MLA_ROPE, 2, dtype=jnp.float32) / MLA_ROPE)
    ang = positions.astype(jnp.float32)[..., None] * inv_freq
    return jnp.cos(ang)[:, :, None, :], jnp.sin(ang)[:, :, None, :]


def _apply_rope(x, cos, sin):
    x1, x2 = jnp.split(x.astype(jnp.float32), 2, axis=-1)
    return jnp.concatenate([x1 * cos - x2 * sin, x1 * sin + x2 * cos], axis=-1).astype(x.dtype)


def mla_mixer(h, positions, w_in, q_a_gain, kv_a_gain, w_q_b, w_kv_b, q_gain, k_gain, w_out):
    bsz, seq, _ = h.shape
    c_q, c_kv, k_rope = jnp.split(h @ w_in, [MLA_Q_RANK, MLA_Q_RANK + MLA_KV_RANK], axis=-1)
    q = (rms_norm(c_q, q_a_gain) @ w_q_b).reshape(bsz, seq, N_HEADS, MLA_NOPE + MLA_ROPE)
    kv = (rms_norm(c_kv, kv_a_gain) @ w_kv_b).reshape(bsz, seq, N_HEADS, MLA_NOPE + MLA_V)
    q_nope, q_rope = jnp.split(q, [MLA_NOPE], axis=-1)
    k_nope, v = jnp.split(kv, [MLA_NOPE], axis=-1)
    cos, sin = _rope_tables(positions)
    q_nope = rms_norm(q_nope, q_gain[:MLA_NOPE])
    k_nope = rms_norm(k_nope, k_gain[:MLA_NOPE])
    q_rope = _apply_rope(rms_norm(q_rope, q_gain[MLA_NOPE:]), cos, sin)
    k_rope = _apply_rope(rms_norm(k_rope[:, :, None, :], k_gain[MLA_NOPE:]), cos, sin)
    q = jnp.concatenate([q_nope, q_rope], axis=-1)
    k = jnp.concatenate([k_nope, jnp.broadcast_to(k_rope, (bsz, seq, N_HEADS, MLA_ROPE))], axis=-1)
    scale = (MLA_NOPE + MLA_ROPE) ** -0.5

    def block(i):
        start = i * Q_BLOCK
        qb = lax.dynamic_slice_in_dim(q, start, Q_BLOCK, axis=1)
        s = jnp.einsum('bqhd,bshd->bhqs', qb, k, preferred_element_type=jnp.float32) * scale
        return _causal_softmax(s, start, v)

    return _sweep_query_blocks(block, seq) @ w_out


def stick_breaking_mixer(h, w_in, q_gain, k_gain, w_out):
    bsz, seq, _ = h.shape
    hd = N_HEADS * HEAD_DIM
    q, k, v = jnp.split(h @ w_in, [hd, 2 * hd], axis=-1)
    q = rms_norm(q.reshape(bsz, seq, N_HEADS, HEAD_DIM), q_gain)
    k = rms_norm(k.reshape(bsz, seq, N_HEADS, HEAD_DIM), k_gain)
    v = v.reshape(bsz, seq, N_HEADS, HEAD_DIM)
    scale = HEAD_DIM ** -0.5

    def block(i):
        start = i * Q_BLOCK
        qb = lax.dynamic_slice_in_dim(q, start, Q_BLOCK, axis=1)
        z = jnp.einsum('bqhd,bshd->bhqs', qb, k, preferred_element_type=jnp.float32) * scale
        q_pos = start + jnp.arange(Q_BLOCK)
        strict = jnp.arange(seq)[None, :] < q_pos[:, None]
        log_keep = jnp.where(strict, -jax.nn.softplus(z), 0.0)
        after = lax.cumsum(log_keep, axis=3, reverse=True) - log_keep
        a = jnp.where(strict, jnp.exp(jax.nn.log_sigmoid(z) + after), 0.0)
        return jnp.einsum('bhqs,bshd->bqhd', a.astype(v.dtype), v)

    return _sweep_query_blocks(block, seq) @ w_out


def sgu_mixer(h, w_in, v_gain, w_s, b_s, w_out):
    bsz, seq, _ = h.shape
    u, vv = jnp.split(jax.nn.gelu(h @ w_in), 2, axis=-1)
    vv = rms_norm(vv, v_gain)
    n_chunks = seq // SGU_CHUNK
    vv = vv.reshape(bsz, n_chunks, SGU_CHUNK, SGU_GROUPS, SGU_GROUP_DIM)
    causal = jnp.tril(jnp.ones((SGU_CHUNK, SGU_CHUNK), dtype=bool))
    ws = jnp.where(causal[None], w_s, 0.0).astype(vv.dtype)
    mixed = jnp.einsum('gts,bnsgc->bntgc', ws, vv) + b_s.T[:, :, None]
    return (u * mixed.reshape(bsz, seq, SGU_WIDTH)) @ w_out


def conv_ffn(h, w_up, conv_w, conv_b, w_down):
    seq = h.shape[1]
    up = h @ w_up
    padded = jnp.pad(up, ((0, 0), (CONV_WIDTH - 1, 0), (0, 0)))
    y = conv_b + conv_w[0] * padded[:, 0:seq]
    for tap in range(1, CONV_WIDTH):
        y = y + conv_w[tap] * padded[:, tap:tap + seq]
    gate, val = jnp.split(y, 2, axis=-1)
    return (jax.nn.silu(gate) * val) @ w_down


def setup_inputs(seed: int = 0) -> dict:
    key = jax.random.key(seed)
    ks = iter(list(jax.random.split(key, 32)))

    def w(shape, fan_in):
        return jax.random.normal(next(ks), shape, jnp.float32) * fan_in ** -0.5

    def gain(shape):
        return 1.0 + 0.1 * jax.random.normal(next(ks), shape, jnp.float32)

    n_a, n_b, n_c, n_d = (_n_layers_of(m) for m in range(N_MIXERS))
    hd = N_HEADS * HEAD_DIM
    x = jax.random.normal(next(ks), (BATCH, SEQ, D_MODEL), jnp.float32)
    offset = jax.random.randint(next(ks), (BATCH, 1), 0, 4096, dtype=jnp.int32)
    positions = offset + jnp.arange(SEQ, dtype=jnp.int32)[None, :]
    return {
        'x': x,
        'positions': positions,
        'mix_norm': gain((DEPTH, D_MODEL)),
        'ffn_norm': gain((DEPTH, D_MODEL)),
        'fox_w_in': w((n_a, D_MODEL, 3 * hd + N_HEADS), D_MODEL),
        'fox_b_f': FOX_FGATE_BIAS + 0.5 * jax.random.normal(next(ks), (n_a, N_HEADS), jnp.float32),
        'fox_q_gain': gain((n_a, HEAD_DIM)),
        'fox_k_gain': gain((n_a, HEAD_DIM)),
        'fox_w_out': w((n_a, hd, D_MODEL), hd),
        'mla_w_in': w((n_b, D_MODEL, MLA_Q_RANK + MLA_KV_RANK + MLA_ROPE), D_MODEL),
        'mla_q_a_gain': gain((n_b, MLA_Q_RANK)),
        'mla_kv_a_gain': gain((n_b, MLA_KV_RANK)),
        'mla_w_q_b': w((n_b, MLA_Q_RANK, N_HEADS * (MLA_NOPE + MLA_ROPE)), MLA_Q_RANK),
        'mla_w_kv_b': w((n_b, MLA_KV_RANK, N_HEADS * (MLA_NOPE + MLA_V)), MLA_KV_RANK),
        'mla_q_gain': gain((n_b, MLA_NOPE + MLA_ROPE)),
        'mla_k_gain': gain((n_b, MLA_NOPE + MLA_ROPE)),
        'mla_w_out': w((n_b, N_HEADS * MLA_V, D_MODEL), N_HEADS * MLA_V),
        'sb_w_in': w((n_c, D_MODEL, 3 * hd), D_MODEL),
        'sb_q_gain': gain((n_c, HEAD_DIM)),
        'sb_k_gain': gain((n_c, HEAD_DIM)),
        'sb_w_out': w((n_c, hd, D_MODEL), hd),
        'sgu_w_in': w((n_d, D_MODEL, 2 * SGU_WIDTH), D_MODEL),
        'sgu_v_gain': gain((n_d, SGU_WIDTH)),
        'sgu_w_s': w((n_d, SGU_GROUPS, SGU_CHUNK, SGU_CHUNK), SGU_CHUNK),
        'sgu_b_s': gain((n_d, SGU_GROUPS, SGU_CHUNK)),
        'sgu_w_out': w((n_d, SGU_WIDTH, D_MODEL), SGU_WIDTH),
        'ffn_w_up': w((DEPTH, D_MODEL, 2 * D_FF), D_MODEL),
        'ffn_conv_w': w((DEPTH, CONV_WIDTH, 2 * D_FF), CONV_WIDTH),
        'ffn_conv_b': 0.01 * jax.random.normal(next(ks), (DEPTH, 2 * D_FF), jnp.float32),
        'ffn_w_down': w((DEPTH, D_FF, D_MODEL), D_FF),
    }


def reference(x, positions, mix_norm, ffn_norm,
              fox_w_in, fox_b_f, fox_q_gain, fox_k_gain, fox_w_out,
              mla_w_in, mla_q_a_gain, mla_kv_a_gain, mla_w_q_b, mla_w_kv_b,
              mla_q_gain, mla_k_gain, mla_w_out,
              sb_w_in, sb_q_gain, sb_k_gain, sb_w_out,
              sgu_w_in, sgu_v_gain, sgu_w_s, sgu_b_s, sgu_w_out,
              ffn_w_up, ffn_conv_w, ffn_conv_b, ffn_w_down):
    h = x
    for i in range(DEPTH):
        m, j = i % N_MIXERS, i // N_MIXERS
        a = rms_norm(h, mix_norm[i])
        if m == 0:
            mixed = fox_mixer(a, fox_w_in[j], fox_b_f[j], fox_q_gain[j], fox_k_gain[j], fox_w_out[j])
        elif m == 1:
            mixed = mla_mixer(a, positions, mla_w_in[j], mla_q_a_gain[j], mla_kv_a_gain[j],
                              mla_w_q_b[j], mla_w_kv_b[j], mla_q_gain[j], mla_k_gain[j], mla_w_out[j])
        elif m == 2:
            mixed = stick_breaking_mixer(a, sb_w_in[j], sb_q_gain[j], sb_k_gain[j], sb_w_out[j])
        else:
            mixed = sgu_mixer(a, sgu_w_in[j], sgu_v_gain[j], sgu_w_s[j], sgu_b_s[j], sgu_w_out[j])
        h = h + mixed
        h = h + conv_ffn(rms_norm(h, ffn_norm[i]), ffn_w_up[i], ffn_conv_w[i], ffn_conv_b[i], ffn_w_down[i])
    return h
```

```python
import contextlib
import numpy as np
import ml_dtypes
import concourse.bass as bass
import concourse.mybir as mybir
from concourse.bass_utils import run_bass_kernel_spmd

F32 = mybir.dt.float32
BF16 = mybir.dt.bfloat16
I32 = mybir.dt.int32
AF = mybir.ActivationFunctionType
ALU = mybir.AluOpType

D = 2048
T = 2048
NH = 16
HD = 128
DFF = 5632
NFC = DFF // 128
EPS = 1e-6
TB = 512
NTB = T // TB
DEPTH = 4

WEIGHT_NAMES = [
    'mix_norm', 'ffn_norm',
    'fox_w_in', 'fox_b_f', 'fox_q_gain', 'fox_k_gain', 'fox_w_out',
    'mla_w_in', 'mla_q_a_gain', 'mla_kv_a_gain', 'mla_w_q_b', 'mla_w_kv_b',
    'mla_q_gain', 'mla_k_gain', 'mla_w_out',
    'sb_w_in', 'sb_q_gain', 'sb_k_gain', 'sb_w_out',
    'sgu_w_in', 'sgu_v_gain', 'sgu_w_s', 'sgu_b_s', 'sgu_w_out',
    'ffn_w_up', 'ffn_conv_w', 'ffn_conv_b', 'ffn_w_down',
]


class Res:
    __slots__ = ("w", "r")

    def __init__(self):
        self.w = None
        self.r = {}


def RL(n):
    return [Res() for _ in range(n)]


class K:
    def __init__(self, nc, es, same_engine_sync=True):
        self.nc = nc
        self.eng = {"pe": nc.tensor, "act": nc.scalar, "dve": nc.vector,
                    "pool": nc.gpsimd, "sp": nc.sync}
        self.sems = {}
        self.cnt = {}
        for e in ("pe", "act", "dve", "pool"):
            self.sems[e] = es.enter_context(nc.semaphore("s_" + e))
            self.cnt[e] = 0
        self.dq = {}
        for q, n in (("sp", 16), ("pool", 8), ("act", 4)):
            lst = []
            for i in range(n):
                key = "d_%s_%d" % (q, i)
                self.sems[key] = es.enter_context(nc.semaphore(key))
                self.cnt[key] = 0
                lst.append(key)
            self.dq[q] = [lst, 0]
        self.seen = {e: {} for e in self.eng}
        self.same = same_engine_sync
        self.ninst = 0

    def _wait(self, e, deps):
        seen = self.seen[e]
        best = {}
        for tok in deps:
            if tok is None:
                continue
            key, val = tok
            if key == e and (e == "pe" or not self.same):
                continue
            if seen.get(key, 0) >= val:
                continue
            if best.get(key, 0) < val:
                best[key] = val
        for key, val in best.items():
            self.eng[e].wait_ge(self.sems[key], val)
            seen[key] = val
            self.ninst += 1

    @staticmethod
    def _deps(reads, writes):
        deps = []
        for r in reads:
            deps.append(r.w)
        for w in writes:
            deps.append(w.w)
            for kk, v in w.r.items():
                deps.append((kk, v))
        return deps

    @staticmethod
    def _commit(tok, reads, writes):
        key, val = tok
        for r in reads:
            if r.r.get(key, 0) < val:
                r.r[key] = val
        for w in writes:
            w.w = tok
            w.r = {}

    def op(self, e, name, reads, writes, *a, **kw):
        self._wait(e, self._deps(reads, writes))
        ins = getattr(self.eng[e], name)(*a, **kw)
        self.cnt[e] += 1
        ins.then_inc(self.sems[e], 1)
        self.ninst += 1
        self._commit((e, self.cnt[e]), reads, writes)

    def mm(self, reads, writes, mms, transpose=False):
        e = "pe"
        self._wait(e, self._deps(reads, writes))
        ins = None
        for m in mms:
            if transpose:
                ins = self.nc.tensor.transpose(m[0], m[1], m[2])
            else:
                if len(m) > 5 and m[5]:
                    ins = self.nc.tensor.matmul(m[0], m[1], m[2], start=m[3], stop=m[4], skip_group_check=True)
                else:
                    ins = self.nc.tensor.matmul(m[0], m[1], m[2], start=m[3], stop=m[4])
            self.ninst += 1
        self.cnt[e] += 1
        ins.then_inc(self.sems[e], 1)
        self._commit((e, self.cnt[e]), reads, writes)

    def dma(self, q, out, in_, reads, writes, **kw):
        lst, idx = self.dq[q]
        key = lst[idx % len(lst)]
        self.dq[q][1] = idx + 1
        deps = self._deps(reads, writes)
        if self.cnt[key] > 0:
            deps.append((key, self.cnt[key]))
        self._wait(q, deps)
        self.cnt[key] += 16
        self.eng[q].dma_start(out=out, in_=in_, **kw).then_inc(self.sems[key], 16)
        self.ninst += 1
        self._commit((key, self.cnt[key]), reads, writes)

    def barrier(self):
        toks = [(key, c) for key, c in self.cnt.items() if c > 0]
        for e in self.eng:
            self._wait(e, [t for t in toks if not (t[0] == e and e == "pe")])


class Buf:
    def __init__(self, t, n=1):
        self.t = t
        self.rs = RL(n)

    @property
    def r(self):
        return self.rs[0]


class NormStream:
    def __init__(self, P, es, pfx):
        self.P = P
        self.aT = [P.sb(es, pfx + "aT%d" % i, [128, 16, TB], BF16, n=16) for i in range(2)]
        self.hc = [P.sb(es, pfx + "hc%d" % i, [128, TB], F32) for i in range(6)]
        self.sq = [P.sb(es, pfx + "sq%d" % i, [128, TB], BF16) for i in range(4)]
        self.rstd = [P.sb(es, pfx + "rs%d" % i, [128, TB], F32) for i in range(2)]
        self.i = 0
        self.n = 0

    def emit(self, tb, gcol0):
        P, k = self.P, self.P.k
        aT = self.aT[self.n % 2]
        rstd = self.rstd[self.n % 2]
        self.n += 1
        hTv = P.hT.rearrange("(c p) t -> p c t", p=128)
        ps = P.banks[7]
        for c in range(16):
            hc = self.hc[self.i % 6]
            sq = self.sq[self.i % 4]
            self.i += 1
            k.dma("sp", hc.t[:], hTv[:, c, tb * TB:(tb + 1) * TB], [P.r_hT[tb][c]], [hc.r])
            k.op("act", "activation", [hc.r], [sq.r], out=sq.t[:], in_=hc.t[:], func=AF.Square)
            k.mm([P.ones_bf.r, sq.r], [ps.r], [(ps.t[:], P.ones_bf.t[:], sq.t[:], c == 0, c == 15)])
        k.op("act", "activation", [ps.r, P.epsc.r], [rstd.r], out=rstd.t[:], in_=ps.t[:], func=AF.Ln, scale=1.0 / D, bias=P.epsc.t[:, 0:1])
        k.op("act", "activation", [rstd.r], [rstd.r], out=rstd.t[:], in_=rstd.t[:], func=AF.Exp, scale=-0.5)
        for c in range(16):
            hc = self.hc[self.i % 6]
            self.i += 1
            k.dma("sp", hc.t[:], hTv[:, c, tb * TB:(tb + 1) * TB], [P.r_hT[tb][c]], [hc.r])
            k.op("dve", "scalar_tensor_tensor", [hc.r, rstd.r, P.cols.r], [aT.rs[c]],
                 out=aT.t[:, c, :], in0=hc.t[:], scalar=P.cols.t[:, gcol0 + c:gcol0 + c + 1], in1=rstd.t[:], op0=ALU.mult, op1=ALU.mult)
        return aT


class Resid:
    def __init__(self, P, es, pfx, n=4):
        self.P = P
        self.ring = [P.sb(es, pfx + "rb%d" % i, [128, TB], F32) for i in range(n)]
        self.i = 0

    def load(self, tb, dc):
        P, k = self.P, self.P.k
        hTv = P.hT.rearrange("(c p) t -> p c t", p=128)
        rb = self.ring[self.i % len(self.ring)]
        self.i += 1
        k.dma("sp", rb.t[:], hTv[:, dc, tb * TB:(tb + 1) * TB], [P.r_hT[tb][dc]], [rb.r])
        return rb

    def add_store(self, rb, tb, dc, ps):
        P, k = self.P, self.P.k
        hTv = P.hT.rearrange("(c p) t -> p c t", p=128)
        k.op("dve", "tensor_tensor", [ps.r, rb.r], [rb.r], rb.t[:], rb.t[:], ps.t[:], ALU.add)
        k.dma("sp", hTv[:, dc, tb * TB:(tb + 1) * TB], rb.t[:], [rb.r], [P.r_hT[tb][dc]])

    def add_only(self, rb, ps):
        k = self.P.k
        k.op("dve", "tensor_tensor", [ps.r, rb.r], [rb.r], rb.t[:], rb.t[:], ps.t[:], ALU.add)

    def emit_out(self, rb, tb, dc, obuf):
        P, k = self.P, self.P.k
        ps = P.bank()
        k.mm([rb.r, P.ident.r], [ps.r],
             [(ps.t[:, ts * 128:(ts + 1) * 128], rb.t[:, ts * 128:(ts + 1) * 128], P.ident.t[:]) for ts in range(4)], transpose=True)
        k.op("act", "copy", [ps.r], [obuf.r], obuf.t[:], ps.t[:])
        r = Res()
        k.dma("sp", P.out[tb * TB:(tb + 1) * TB, dc * 128:(dc + 1) * 128].rearrange("(s p) d -> p s d", p=128),
              obuf.t[:].rearrange("p (s d) -> p s d", s=4), [obuf.r], [r])


class Prog:
    def __init__(self, layers, do_in=True, do_out=True, debug_out=None):
        self.layers = layers
        self.fuse_out = (layers is None) or ((DEPTH - 1) in layers and 'ffn' in layers[DEPTH - 1])
        self.nc = nc = bass.Bass("TRN2", target_bir_lowering=False)
        self.din = {}
        self.w = {}

    def declare(self, shapes):
        nc = self.nc
        self.x = nc.dram_tensor("x", [T, D], F32, kind="ExternalInput").ap()
        self.pos = nc.dram_tensor("positions", [T], I32, kind="ExternalInput").ap()
        for name in WEIGHT_NAMES:
            self.w[name] = nc.dram_tensor(name, list(shapes[name]), F32, kind="ExternalInput").ap()
        for name in ("c_ident", "c_consts"):
            pass
        self.c_ident = nc.dram_tensor("c_ident", [128, 128], F32, kind="ExternalInput").ap()
        self.c_tri = nc.dram_tensor("c_tri", [4, 128, 128], F32, kind="ExternalInput").ap()
        dbg = {"kind": "ExternalOutput"} if getattr(self, "debug", False) else {}
        self.qT = nc.dram_tensor("qT_scr", [NH, 192, T], BF16, **dbg).ap()
        self.kT = nc.dram_tensor("kT_scr", [NH, 128, T], BF16, **dbg).ap()
        self.V = nc.dram_tensor("V_scr", [T, NH * HD], BF16, **dbg).ap()
        self.aoT = nc.dram_tensor("aoT_scr", [NH * HD, T], BF16).ap()
        self.augQ = nc.dram_tensor("augQ_scr", [NH, 6, T], BF16).ap()
        self.augK = nc.dram_tensor("augK_scr", [NH, 6, T], BF16).ap()
        self.r_q, self.r_k, self.r_v, self.r_ao = RL(NTB), RL(NTB), RL(NTB), RL(NH)
        self.r_aug = RL(12)
        self.kropeT = nc.dram_tensor("krope_scr", [64, T], BF16, **dbg).ap()
        self.r_kr = RL(NTB)
        self.c_perm = nc.dram_tensor("c_perm", [64, 64], F32, kind="ExternalInput").ap()
        self.c_rope = nc.dram_tensor("c_rope", [64, 2], F32, kind="ExternalInput").ap()
        self.out = nc.dram_tensor("out", [T, D], F32, kind="ExternalOutput").ap()
        self.hT = nc.dram_tensor("hT_scr", [D, T], F32).ap()
        self.r_hT = [RL(16) for _ in range(NTB)]

    def sb(self, es, name, shape, dt=F32, n=1):
        self.uid = getattr(self, "uid", 0) + 1
        t = es.enter_context(self.nc.sbuf_tensor("%s_%d" % (name, self.uid), shape, dt))
        return Buf(t, n)

    def bank(self):
        b = self.banks[self.bank_i % 7]
        self.bank_i += 1
        return b

    def wslot(self):
        s = self.wslots[self.wslot_i % len(self.wslots)]
        self.wslot_i += 1
        return s

    def build(self, shapes):
        nc = self.nc
        self.declare(shapes)
        with contextlib.ExitStack() as es:
            self.k = k = K(nc, es)
            self.banks = [Buf(es.enter_context(nc.psum_tensor("bank%d" % i, [128, 512], F32))) for i in range(8)]
            self.bank_i = 0
            self.ident = self.sb(es, "ident", [128, 128], F32)
            k.dma("sp", self.ident.t[:], self.c_ident[:, :], [], [self.ident.r])
            self.ones_bf = self.sb(es, "ones_bf", [128, 128], BF16)
            k.op("dve", "memset", [], [self.ones_bf.r], self.ones_bf.t[:], 1.0)
            self.zbf = self.sb(es, "zbf", [128, 128], BF16)
            k.op("dve", "memset", [], [self.zbf.r], self.zbf.t[:], 0.0)
            self.epsc = self.sb(es, "epsc", [128, 1], F32)
            k.op("dve", "memset", [], [self.epsc.r], self.epsc.t[:], EPS)
            self.tri = self.sb(es, "tri", [128, 4, 128], BF16)
            k.dma("pool", self.tri.t[:], self.c_tri.rearrange("a p f -> p a f"), [], [self.tri.r])
            self.cols = self.sb(es, "cols", [128, 2 * 64 + 4 * 264 + 352], F32)
            self.C_MIX, self.C_FFN, self.C_CW, self.C_CB = 0, 64, 128, 128 + 4 * 264
            with contextlib.ExitStack() as pes:
                stg = self.sb(pes, "stg", [128, 128], F32)
                self.load_cols(stg, self.w['mix_norm'].rearrange("l (c p) -> (l c) p", p=128), 64, self.C_MIX)
                self.load_cols(stg, self.w['ffn_norm'].rearrange("l (c p) -> (l c) p", p=128), 64, self.C_FFN)
                self.load_cols(stg, self.w['ffn_conv_w'].rearrange("l t (c p) -> (l t c) p", p=128), 4 * 264, self.C_CW)
                self.load_cols(stg, self.w['ffn_conv_b'].rearrange("l (c p) -> (l c) p", p=128), 352, self.C_CB)
                k.barrier()
            if self.layers is None or self.layers.get('in', True):
                with contextlib.ExitStack() as pes:
                    self.phase_in(pes)
                    k.barrier()
            for L in range(DEPTH):
                cfg = None if self.layers is None else self.layers.get(L)
                if self.layers is not None and cfg is None:
                    continue
                if cfg is None or 'mix' in cfg:
                    m = L % 4
                    if m == 3:
                        with contextlib.ExitStack() as pes:
                            self.phase_sgu(pes, L)
                            k.barrier()
                    else:
                        kind = ('fox', 'mla', 'sb')[m]
                        with contextlib.ExitStack() as pes:
                            if kind == 'mla':
                                self.phase_mla_proj(pes, L)
                            else:
                                self.phase_qkv(pes, L, kind)
                            k.barrier()
                        with contextlib.ExitStack() as pes:
                            if kind == 'sb':
                                self.phase_attn_sb(pes, L)
                            else:
                                self.phase_attn(pes, L, kind)
                            k.barrier()
                        with contextlib.ExitStack() as pes:
                            self.phase_oproj(pes, self.w[kind + '_w_out'][L // 4])
                            k.barrier()
                if cfg is None or 'ffn' in cfg:
                    with contextlib.ExitStack() as pes:
                        self.phase_ffn(pes, L)
                        k.barrier()
            if (self.layers is None or self.layers.get('out', True)) and not self.fuse_out:
                with contextlib.ExitStack() as pes:
                    self.phase_out(pes)
                    k.barrier()
        return nc

    def load_cols(self, stg, src, n, col0):
        k, nc = self.k, self.nc
        i = 0
        while i < n:
            m = min(128, n - i)
            k.dma("sp", stg.t[0:m, :], src[i:i + m, :], [], [stg.r])
            ps = self.bank()
            k.mm([stg.r, self.ident.r], [ps.r], [(ps.t[:, 0:m], stg.t[0:m, :], self.ident.t[0:m, 0:m])], transpose=True)
            k.op("dve", "tensor_copy", [ps.r], [self.cols.r], self.cols.t[:, col0 + i:col0 + i + m], ps.t[:, 0:m])
            i += m

    def rmsnorm(self, hb, gcol0, aT, sq, tmp):
        k, nc = self.k, self.nc
        ps = self.bank()
        for g4 in range(4):
            for c in range(4 * g4, 4 * g4 + 4):
                k.op("act", "activation", [hb.rs[c]], [sq.rs[c % 4]], out=sq.t[:, c % 4, :], in_=hb.t[:, c, :], func=AF.Square)
            k.mm([self.ones_bf.r] + sq.rs, [ps.r],
                 [(ps.t[:], self.ones_bf.t[:], sq.t[:, c % 4, :], c == 0, c == 15) for c in range(4 * g4, 4 * g4 + 4)])
        k.op("act", "activation", [ps.r, self.epsc.r], [tmp.r], out=tmp.t[:], in_=ps.t[:], func=AF.Sqrt, scale=1.0 / D, bias=self.epsc.t[:, 0:1])
        k.op("dve", "reciprocal", [tmp.r], [tmp.r], tmp.t[:], tmp.t[:])
        for c in range(16):
            k.op("dve", "scalar_tensor_tensor", [hb.rs[c], tmp.r, self.cols.r], [aT.rs[c]],
                 out=aT.t[:, c, :], in0=hb.t[:, c, :], scalar=self.cols.t[:, gcol0 + c:gcol0 + c + 1], in1=tmp.t[:],
                 op0=ALU.mult, op1=ALU.mult)

    def phase_in(self, es):
        k, nc = self.k, self.nc
        xin = [self.sb(es, "xin%d" % i, [128, D], F32) for i in range(2)]
        hb = [self.sb(es, "hin%d" % i, [128, 16, TB], F32, n=16) for i in range(2)]
        hTv = self.hT.rearrange("(c p) t -> p c t", p=128)
        for tb in range(NTB):
            h = hb[tb % 2]
            for ts in range(4):
                xi = xin[(tb * 4 + ts) % 2]
                t0 = tb * TB + ts * 128
                k.dma("sp", xi.t[:], self.x[t0:t0 + 128, :], [], [xi.r])
                for dc4 in range(4):
                    ps = self.bank()
                    k.mm([xi.r, self.ident.r], [ps.r],
                         [(ps.t[:, j * 128:(j + 1) * 128], xi.t[:, (dc4 * 4 + j) * 128:(dc4 * 4 + j + 1) * 128], self.ident.t[:]) for j in range(4)],
                         transpose=True)
                    eng = "dve" if dc4 % 2 == 0 else "act"
                    rs = [h.rs[dc4 * 4 + j] for j in range(4)]
                    if eng == "dve":
                        k.op("dve", "tensor_copy", [ps.r], rs, h.t[:, dc4 * 4:dc4 * 4 + 4, ts * 128:(ts + 1) * 128],
                             ps.t[:].rearrange("p (j t) -> p j t", j=4))
                    else:
                        k.op("act", "copy", [ps.r], rs, h.t[:, dc4 * 4:dc4 * 4 + 4, ts * 128:(ts + 1) * 128],
                             ps.t[:].rearrange("p (j t) -> p j t", j=4))
            k.dma("pool", hTv[:, :, tb * TB:(tb + 1) * TB], h.t[:], h.rs, self.r_hT[tb])

    def phase_out(self, es):
        k, nc = self.k, self.nc
        hb = [self.sb(es, "hout%d" % i, [128, 16, TB], F32) for i in range(2)]
        ob = [self.sb(es, "oout%d" % i, [128, D], F32, n=4) for i in range(2)]
        hTv = self.hT.rearrange("(c p) t -> p c t", p=128)
        ro = Res()
        self.r_out = []
        for tb in range(NTB):
            h = hb[tb % 2]
            k.dma("sp", h.t[:], hTv[:, :, tb * TB:(tb + 1) * TB], self.r_hT[tb], [h.r])
            for ts in range(4):
                o = ob[(tb * 4 + ts) % 2]
                for dc4 in range(4):
                    ps = self.bank()
                    k.mm([h.r, self.ident.r], [ps.r],
                         [(ps.t[:, j * 128:(j + 1) * 128], h.t[:, dc4 * 4 + j, ts * 128:(ts + 1) * 128], self.ident.t[:]) for j in range(4)],
                         transpose=True)
                    if dc4 % 2 == 0:
                        k.op("dve", "tensor_copy", [ps.r], [o.rs[dc4]], o.t[:, dc4 * 512:(dc4 + 1) * 512], ps.t[:])
                    else:
                        k.op("act", "copy", [ps.r], [o.rs[dc4]], o.t[:, dc4 * 512:(dc4 + 1) * 512], ps.t[:])
                t0 = tb * TB + ts * 128
                r = Res()
                k.dma("sp", self.out[t0:t0 + 128, :], o.t[:], o.rs, [r])
                self.r_out.append(r)

    def phase_ffn(self, es, L):
        k, nc = self.k, self.nc
        w_up = self.w['ffn_w_up'][L].rearrange("(c p) f -> p c f", p=128)
        w_dn = self.w['ffn_w_down'][L].rearrange("(c p) d -> p c d", p=128)
        hTv = self.hT.rearrange("(c p) t -> p c t", p=128)
        ns = NormStream(self, es, "f_")
        final = (L == DEPTH - 1) and self.fuse_out
        rsd = Resid(self, es, "f_", n=4)
        pend, oi = [], [0]
        obufs = [self.sb(es, "f_ob%d" % i, [128, TB], F32) for i in range(2)] if final else []
        gT = self.sb(es, "f_gT", [128, NFC, TB], BF16, n=NFC)
        self.wslots = [self.sb(es, "f_w%d" % i, [128, NFC * 256], BF16, n=2) for i in range(3)]
        self.wslot_i = 0
        halo = self.sb(es, "f_halo", [128, 2 * NFC, 2], F32, n=2 * NFC)
        U = [[self.sb(es, "f_U%d%d" % (i, j), [128, TB + 2], F32, n=2) for j in range(2)] for i in range(2)]
        Y = [[self.sb(es, "f_Y%d%d" % (i, j), [128, TB], F32) for j in range(2)] for i in range(2)]
        SG = [self.sb(es, "f_S%d" % i, [128, TB], F32) for i in range(2)]
        cw0 = self.C_CW + L * 264
        cb0 = self.C_CB + L * 88
        cols = self.cols
        pair_i = 0
        aT_next = ns.emit(0, self.C_FFN + L * 16)
        for tb in range(NTB):
            aT = aT_next
            for pg in range(NFC // 2):
                slot = self.wslot()
                sv = slot.t[:, 0:8192].rearrange("p (c j f) -> p c j f", c=16, j=2)
                k.dma("pool", sv[:, :, 0, :], w_up[:, :, pg * 256:(pg + 1) * 256], [], [slot.rs[0]])
                k.dma("pool", sv[:, :, 1, :], w_up[:, :, DFF + pg * 256:DFF + (pg + 1) * 256], [], [slot.rs[1]])
                for j in range(2):
                    fc = pg * 2 + j
                    pss = [self.bank(), self.bank()]
                    for gv in range(2):
                        k.mm([slot.rs[gv]] + aT.rs, [pss[gv].r],
                             [(pss[gv].t[:], sv[:, kc, gv, j * 128:(j + 1) * 128], aT.t[:, kc, :], kc == 0, kc == 15) for kc in range(16)])
                    ys = []
                    for gv in range(2):
                        ch = gv * NFC + fc
                        u = U[pair_i % 2][gv]
                        y = Y[pair_i % 2][gv]
                        k.op("act", "copy", [pss[gv].r], [u.rs[0]], u.t[:, 2:TB + 2], pss[gv].t[:])
                        if tb == 0:
                            k.op("dve", "memset", [], [u.rs[1]], u.t[:, 0:2], 0.0)
                        else:
                            k.op("dve", "tensor_copy", [halo.rs[ch]], [u.rs[1]], u.t[:, 0:2], halo.t[:, ch, :])
                        k.op("act", "activation", [u.rs[0], cols.r], [y.r], out=y.t[:], in_=u.t[:, 2:TB + 2], func=AF.Identity,
                             scale=cols.t[:, cw0 + 2 * 88 + ch:cw0 + 2 * 88 + ch + 1], bias=cols.t[:, cb0 + ch:cb0 + ch + 1])
                        k.op("dve", "scalar_tensor_tensor", [u.rs[0], u.rs[1], cols.r, y.r], [y.r],
                             out=y.t[:], in0=u.t[:, 1:TB + 1], scalar=cols.t[:, cw0 + 88 + ch:cw0 + 88 + ch + 1], in1=y.t[:],
                             op0=ALU.mult, op1=ALU.add)
                        k.op("dve", "scalar_tensor_tensor", [u.rs[0], u.rs[1], cols.r, y.r], [y.r],
                             out=y.t[:], in0=u.t[:, 0:TB], scalar=cols.t[:, cw0 + ch:cw0 + ch + 1], in1=y.t[:],
                             op0=ALU.mult, op1=ALU.add)
                        if tb < NTB - 1:
                            k.op("dve", "tensor_copy", [u.rs[0]], [halo.rs[ch]], halo.t[:, ch, :], u.t[:, TB:TB + 2])
                        ys.append(y)
                    sg = SG[pair_i % 2]
                    k.op("act", "activation", [ys[0].r], [sg.r], out=sg.t[:], in_=ys[0].t[:], func=AF.Silu)
                    k.op("dve", "tensor_tensor", [sg.r, ys[1].r], [gT.rs[fc]], gT.t[:, fc, :], sg.t[:], ys[1].t[:], ALU.mult)
                    pair_i += 1
            if tb + 1 < NTB:
                aT_next = ns.emit(tb + 1, self.C_FFN + L * 16)
            for dg in range(8):
                slot = self.wslot()
                sv = slot.t[:].rearrange("p (c d) -> p c d", c=NFC)
                k.dma("pool", sv, w_dn[:, :, dg * 256:(dg + 1) * 256], [], [slot.rs[0], slot.rs[1]])
                for dd in range(2):
                    dc = dg * 2 + dd
                    ps = self.bank()
                    rb = rsd.load(tb, dc)
                    k.mm(slot.rs + gT.rs, [ps.r],
                         [(ps.t[:], sv[:, c, dd * 128:(dd + 1) * 128], gT.t[:, c, :], c == 0, c == NFC - 1) for c in range(NFC)])
                    if not final:
                        rsd.add_store(rb, tb, dc, ps)
                    else:
                        if pend:
                            rsd.emit_out(*pend.pop())
                        rsd.add_only(rb, ps)
                        pend.append((rb, tb, dc, obufs[oi[0] % len(obufs)]))
                        oi[0] += 1
            if final and pend:
                rsd.emit_out(*pend.pop())

    def qk_norm(self, ps, gcol, out_ap, out_rs, sqh, rt, nrows=128):
        k = self.k
        k.op("act", "activation", [ps.r], [sqh.r], out=sqh.t[0:nrows, :], in_=ps.t[0:nrows, :], func=AF.Square)
        ps2 = self.bank()
        k.mm([self.ones_bf.r, sqh.r], [ps2.r], [(ps2.t[0:nrows, :], self.ones_bf.t[0:nrows, 0:nrows], sqh.t[0:nrows, :], True, True)])
        k.op("act", "activation", [ps2.r, self.epsc.r], [rt.r], out=rt.t[0:nrows, :], in_=ps2.t[0:nrows, :], func=AF.Sqrt,
             scale=1.0 / nrows, bias=self.epsc.t[0:nrows, 0:1])
        k.op("dve", "reciprocal", [rt.r], [rt.r], rt.t[0:nrows, :], rt.t[0:nrows, :])
        k.op("dve", "scalar_tensor_tensor", [ps.r, rt.r] + gcol[1], out_rs,
             out=out_ap, in0=ps.t[0:nrows, :], scalar=gcol[0], in1=rt.t[0:nrows, :], op0=ALU.mult, op1=ALU.mult)


    def norm_pipeline(self, items, sqh, rt, xr=None, t1=None, rope=None):
        k, banks = self.k, self.banks
        for i, it in enumerate(items):
            it['ps'] = banks[i % 4]
            it['ps2'] = banks[4 + i % 2]
            it['psc'] = banks[6]
            it['sqh'] = sqh[i % len(sqh)]
            it['rt'] = rt[i % len(rt)]
            if it.get('rope'):
                it['xr'] = xr[i % len(xr)]
                it['t1'] = t1[i % len(t1)]

        def s0(it):
            if it.get('pre'):
                it['pre']()
            k.mm(it['readsf'](), [it['ps'].r], it['mmf'](it['ps']))

        def s1(it):
            n = it['nrows']
            k.op("act", "activation", [it['ps'].r], [it['sqh'].r], out=it['sqh'].t[0:n, :], in_=it['ps'].t[0:n, :], func=AF.Square)

        def s2(it):
            n = it['nrows']
            k.mm([self.ones_bf.r, it['sqh'].r], [it['ps2'].r], [(it['ps2'].t[0:n, :], self.ones_bf.t[0:n, 0:n], it['sqh'].t[0:n, :], True, True)])

        def s3(it):
            n = it['nrows']
            r_ = it['rt']
            k.op("act", "activation", [it['ps2'].r, self.epsc.r], [r_.r], out=r_.t[0:n, :], in_=it['ps2'].t[0:n, :], func=AF.Ln,
                 scale=1.0 / n, bias=self.epsc.t[0:n, 0:1])
            k.op("act", "activation", [r_.r], [r_.r], out=r_.t[0:n, :], in_=r_.t[0:n, :], func=AF.Exp, scale=-0.5)
            if it.get('rope'):
                dst, dst_rs = it['xr'].t[:], [it['xr'].r]
            else:
                dst, dst_rs = it['dst'], it['dst_rs']
            k.op("dve", "scalar_tensor_tensor", [it['ps'].r, r_.r] + it['gcol'][1], dst_rs,
                 out=dst, in0=it['ps'].t[0:n, :], scalar=it['gcol'][0], in1=r_.t[0:n, :], op0=ALU.mult, op1=ALU.mult)
            if not it.get('rope') and it.get('post'):
                it['post']()

        def s4(it):
            if it.get('rope'):
                perm, cos2, sin2 = rope
                x = it['xr']
                k.mm([perm.r, x.r], [it['psc'].r], [(it['psc'].t[0:64, :], perm.t[:], x.t[:], True, True)])
                k.op("dve", "tensor_tensor", [x.r, cos2.r], [it['t1'].r], it['t1'].t[:], x.t[:], cos2.t[:], ALU.mult)

        def s5(it):
            if it.get('rope'):
                perm, cos2, sin2 = rope
                x = it['xr']
                k.op("dve", "tensor_tensor", [it['psc'].r, sin2.r], [x.r], x.t[:], it['psc'].t[0:64, :], sin2.t[:], ALU.mult)
                k.op("dve", "tensor_tensor", [x.r, it['t1'].r], it['dst_rs'], it['dst'], x.t[:], it['t1'].t[:], ALU.add)
                if it.get('post'):
                    it['post']()

        self.pipeline(items, [(0, s0), (1, s1), (2, s2), (3, s3), (4, s4), (5, s5)])

    def phase_qkv(self, es, L, kind):
        k, nc = self.k, self.nc
        j = L // 4
        w_in = self.w[kind + '_w_in'][j].rearrange("(c p) f -> p c f", p=128)
        hTv = self.hT.rearrange("(c p) t -> p c t", p=128)
        gq = self.sb(es, "q_gq", [128, 2], F32)
        k.dma("sp", gq.t[:, 0:1], self.w[kind + '_q_gain'][j].rearrange("(p o) -> p o", o=1), [], [gq.r])
        k.dma("sp", gq.t[:, 1:2], self.w[kind + '_k_gain'][j].rearrange("(p o) -> p o", o=1), [], [gq.r])
        k.op("dve", "tensor_scalar", [gq.r], [gq.r], gq.t[:, 0:1], gq.t[:, 0:1], float(HD) ** -0.5, None, op0=ALU.mult)
        if kind == 'fox':
            nbf = self.sb(es, "q_nbf", [16, 1], F32)
            k.dma("sp", nbf.t[:], self.w['fox_b_f'][j].rearrange("(p o) -> p o", o=1), [], [nbf.r])
            k.op("dve", "tensor_scalar", [nbf.r], [nbf.r], nbf.t[:], nbf.t[:], -1.0, None, op0=ALU.mult)
            cumN = self.sb(es, "q_cum", [16, T], F32, n=NTB)
            ones16 = self.sb(es, "q_ones16", [16, TB], F32)
            k.op("dve", "memset", [], [ones16.r], ones16.t[:], 1.0)
            fe = self.sb(es, "q_fe", [16, TB], F32)
            wf = self.sb(es, "q_wf", [128, 16, 16], BF16)
            k.dma("pool", wf.t[:], w_in[:, :, 3 * NH * HD:3 * NH * HD + 16], [], [wf.r])
        with contextlib.ExitStack() as es2:
            ns = NormStream(self, es2, "q_")
            self.wslots = [self.sb(es2, "q_w%d" % i, [128, 8192], BF16, n=2) for i in range(3)]
            self.wslot_i = 0
            sqh = [self.sb(es2, "q_sqh%d" % i, [128, TB], BF16) for i in range(3)]
            rt = [self.sb(es2, "q_rt%d" % i, [128, TB], F32) for i in range(2)]
            stage = [self.sb(es2, "q_st%d" % i, [128, NH, TB], BF16, n=NH) for i in range(2)]
            vst = self.sb(es2, "q_vst", [128, 4, NH * HD], BF16, n=16)
            qi = 0
            aT_next = ns.emit(0, self.C_MIX + L * 16)
            for tb in range(NTB):
                aT = aT_next
                groups = [(which, hg) for which in range(2) for hg in range(4)]
                slots = {}

                def load_group(gi):
                    which, hg = groups[gi]
                    slot = self.wslot()
                    sv = slot.t[:].rearrange("p (c f) -> p c f", c=16)
                    c0 = which * NH * HD + hg * 512
                    k.dma("pool", sv, w_in[:, :, c0:c0 + 512], [], slot.rs)
                    slots[gi] = (slot, sv)

                load_group(0)
                load_group(1)
                items = []
                for gi, (which, hg) in enumerate(groups):
                    st = stage[which]
                    for hh in range(4):
                        h = hg * 4 + hh
                        it = dict(nrows=128, gcol=(gq.t[:, which:which + 1], [gq.r]), dst=st.t[:, h, :], dst_rs=[st.rs[h]])
                        if hh == 0 and gi + 2 < len(groups):
                            it['pre'] = (lambda gi=gi: load_group(gi + 2))
                        it['mmf'] = (lambda ps, gi=gi, hh=hh: [(ps.t[:], slots[gi][1][:, kc, hh * 128:(hh + 1) * 128], aT.t[:, kc, :], kc == 0, kc == 15)
                                                             for kc in range(16)])
                        it['readsf'] = (lambda gi=gi: slots[gi][0].rs + aT.rs)
                        if h == NH - 1:
                            if which == 0:
                                it['post'] = (lambda st=st, tb=tb: k.dma("sp", self.qT[:, 0:128, tb * TB:(tb + 1) * TB].rearrange("h p t -> p h t"),
                                                                         st.t[:], st.rs, [self.r_q[tb]]))
                            else:
                                it['post'] = (lambda st=st, tb=tb: k.dma("sp", self.kT[:, :, tb * TB:(tb + 1) * TB].rearrange("h p t -> p h t"),
                                                                         st.t[:], st.rs, [self.r_k[tb]]))
                        items.append(it)
                self.norm_pipeline(items, sqh, rt)
                if tb + 1 < NTB:
                    aT_next = ns.emit(tb + 1, self.C_MIX + L * 16)
                for vc in range(4):
                    slot = self.wslot()
                    sv = slot.t[:].rearrange("p (c f) -> p c f", c=16)
                    c0 = 2 * NH * HD + vc * 512
                    k.dma("pool", sv, w_in[:, :, c0:c0 + 512], [], slot.rs)
                    for ts in range(4):
                        ps = self.bank()
                        k.mm(slot.rs + aT.rs, [ps.r],
                             [(ps.t[:], aT.t[:, kc, ts * 128:(ts + 1) * 128], sv[:, kc, :], kc == 0, kc == 15) for kc in range(16)])
                        if (vc + ts) % 2 == 0:
                            k.op("act", "copy", [ps.r], [vst.rs[ts * 4 + vc]], vst.t[:, ts, vc * 512:(vc + 1) * 512], ps.t[:])
                        else:
                            k.op("dve", "tensor_copy", [ps.r], [vst.rs[ts * 4 + vc]], vst.t[:, ts, vc * 512:(vc + 1) * 512], ps.t[:])
                k.dma("sp", self.V[tb * TB:(tb + 1) * TB, :].rearrange("(s p) f -> p s f", p=128), vst.t[:], vst.rs, [self.r_v[tb]])
                if kind == 'fox':
                    ps = self.bank()
                    k.mm([wf.r] + aT.rs, [ps.r], [(ps.t[0:16, :], wf.t[:, kc, :], aT.t[:, kc, :], kc == 0, kc == 15) for kc in range(16)])
                    k.op("act", "activation", [ps.r, nbf.r], [fe.r], out=fe.t[:], in_=ps.t[0:16, :], func=AF.Exp, scale=-1.0, bias=nbf.t[:, 0:1])
                    k.op("act", "activation", [fe.r], [fe.r], out=fe.t[:], in_=fe.t[:], func=AF.Ln, bias=1.0)
                    if tb == 0:
                        k.op("dve", "tensor_tensor_scan", [fe.r, ones16.r], [cumN.rs[tb]], out=cumN.t[:, 0:TB], data0=ones16.t[:], data1=fe.t[:],
                             initial=0.0, op0=ALU.mult, op1=ALU.add)
                    else:
                        k.op("dve", "tensor_tensor_scan", [fe.r, ones16.r, cumN.rs[tb - 1]], [cumN.rs[tb]], out=cumN.t[:, tb * TB:(tb + 1) * TB],
                             data0=ones16.t[:], data1=fe.t[:], initial=cumN.t[:, tb * TB - 1:tb * TB], op0=ALU.mult, op1=ALU.add)
            k.barrier()
        if kind == 'fox':
            res_ = self.sb(es, "q_res", [16, T], F32)
            cs = [self.sb(es, "q_cs%d" % i, [16, T], BF16) for i in range(3)]
            ncs = [self.sb(es, "q_ncs%d" % i, [16, T], BF16) for i in range(3)]
            onesr = self.sb(es, "q_onesr", [16, T], BF16)
            k.op("dve", "memset", [], [onesr.r], onesr.t[:], 1.0)
            src, src_r = cumN.t, cumN.rs
            for i in range(3):
                k.op("dve", "tensor_copy", src_r, [cs[i].r], cs[i].t[:], src[:])
                k.op("dve", "tensor_scalar", [cs[i].r], [ncs[i].r], ncs[i].t[:], cs[i].t[:], -1.0, None, op0=ALU.mult)
                if i < 2:
                    k.op("dve", "tensor_tensor", src_r + [cs[i].r], [res_.r], res_.t[:], src[:], cs[i].t[:], ALU.subtract)
                    src, src_r = res_.t, [res_.r]
            for i in range(3):
                k.dma("sp", self.augQ[:, i, :], ncs[i].t[:], [ncs[i].r], [self.r_aug[i]])
                k.dma("sp", self.augQ[:, 3 + i, :], onesr.t[:], [onesr.r], [self.r_aug[3 + i]])
                k.dma("sp", self.augK[:, i, :], onesr.t[:], [onesr.r], [self.r_aug[6 + i]])
                k.dma("sp", self.augK[:, 3 + i, :], cs[i].t[:], [cs[i].r], [self.r_aug[9 + i]])

    def phase_mla_proj(self, es, L):
        k, nc = self.k, self.nc
        j = L // 4
        w_in = self.w['mla_w_in'][j].rearrange("(c p) f -> p c f", p=128)
        w_qb = self.w['mla_w_q_b'][j].rearrange("(c p) f -> p c f", p=128)
        w_kvb = self.w['mla_w_kv_b'][j].rearrange("(c p) f -> p c f", p=128)
        SC = 192.0 ** -0.5
        ns = NormStream(self, es, "m_")
        sq = self.sb(es, "m_sq", [128, 4, TB], BF16, n=4)
        tmp = self.sb(es, "m_tmp", [128, TB], F32)
        self.wslots = [self.sb(es, "m_w%d" % i, [128, 8192], BF16, n=2) for i in range(2)]
        self.wslot_i = 0
        sqh = [self.sb(es, "m_sqh%d" % i, [128, TB], BF16) for i in range(3)]
        rt = [self.sb(es, "m_rt%d" % i, [128, TB], F32) for i in range(2)]
        lat = [self.sb(es, "m_lat%d" % i, [128, 4, TB], BF16, n=4) for i in range(2)]
        qst = self.sb(es, "m_qst", [128, NH, TB], BF16, n=NH)
        kst = self.sb(es, "m_kst", [128, NH, TB], BF16, n=NH)
        qrst = self.sb(es, "m_qrst", [64, NH, TB], BF16, n=NH)
        krst = self.sb(es, "m_krst", [64, TB], BF16)
        vst = self.sb(es, "m_vst", [128, 4, NH * HD], BF16, n=16)
        xr = [self.sb(es, "m_xr%d" % i, [64, TB], F32) for i in range(4)]
        t1 = [self.sb(es, "m_t1%d" % i, [64, TB], F32) for i in range(3)]
        g = self.sb(es, "m_g", [128, 16], F32)
        k.dma("sp", g.t[:, 0:4], self.w['mla_q_a_gain'][j].rearrange("(c p) -> p c", p=128), [], [g.r], allow_slow_non_contiguous=True)
        k.dma("sp", g.t[:, 4:8], self.w['mla_kv_a_gain'][j].rearrange("(c p) -> p c", p=128), [], [g.r], allow_slow_non_contiguous=True)
        k.dma("sp", g.t[:, 8:9], self.w['mla_q_gain'][j][0:128].rearrange("(p o) -> p o", o=1), [], [g.r])
        k.dma("sp", g.t[0:64, 9:10], self.w['mla_q_gain'][j][128:192].rearrange("(p o) -> p o", o=1), [], [g.r])
        k.dma("sp", g.t[:, 10:11], self.w['mla_k_gain'][j][0:128].rearrange("(p o) -> p o", o=1), [], [g.r])
        k.dma("sp", g.t[0:64, 11:12], self.w['mla_k_gain'][j][128:192].rearrange("(p o) -> p o", o=1), [], [g.r])
        k.op("dve", "tensor_scalar", [g.r], [g.r], g.t[:, 8:9], g.t[:, 8:9], SC, None, op0=ALU.mult)
        k.op("dve", "tensor_scalar", [g.r], [g.r], g.t[0:64, 9:10], g.t[0:64, 9:10], SC, None, op0=ALU.mult)
        perm = self.sb(es, "m_perm", [64, 64], F32)
        k.dma("sp", perm.t[:], self.c_perm[:, :], [], [perm.r])
        rc = self.sb(es, "m_rc", [64, 2], F32)
        k.dma("sp", rc.t[:], self.c_rope[:, :], [], [rc.r])
        posi = self.sb(es, "m_posi", [64, TB], I32)
        ang = self.sb(es, "m_ang", [64, TB], F32)
        kf = self.sb(es, "m_kf", [64, TB], F32)
        ki = self.sb(es, "m_ki", [64, TB], I32)
        rr = self.sb(es, "m_rr", [64, TB], F32)
        cos2 = self.sb(es, "m_cos", [64, TB], F32)
        sin2 = self.sb(es, "m_sin", [64, TB], F32)
        TWO_PI = 6.283185307179586
        C1 = 6.28125
        C2 = TWO_PI - C1
        PI_S = 3.1415925
        HALF_PI = 1.5707963267948966

        def latent(aT, col0, gcol0, dst):
            slot = self.wslot()
            sv = slot.t[:].rearrange("p (c f) -> p c f", c=16)
            k.dma("pool", sv, w_in[:, :, col0:col0 + 512], [], slot.rs)
            pss = [self.bank() for _ in range(4)]
            for cc in range(4):
                k.mm(slot.rs + aT.rs, [pss[cc].r],
                     [(pss[cc].t[:], sv[:, kc, cc * 128:(cc + 1) * 128], aT.t[:, kc, :], kc == 0, kc == 15) for kc in range(16)])
            ps2 = self.bank()
            for cc in range(4):
                k.op("act", "activation", [pss[cc].r], [sq.rs[cc]], out=sq.t[:, cc, :], in_=pss[cc].t[:], func=AF.Square)
            k.mm([self.ones_bf.r] + sq.rs, [ps2.r], [(ps2.t[:], self.ones_bf.t[:], sq.t[:, cc, :], cc == 0, cc == 3) for cc in range(4)])
            k.op("act", "activation", [ps2.r, self.epsc.r], [tmp.r], out=tmp.t[:], in_=ps2.t[:], func=AF.Ln, scale=1.0 / 512, bias=self.epsc.t[:, 0:1])
            k.op("act", "activation", [tmp.r], [tmp.r], out=tmp.t[:], in_=tmp.t[:], func=AF.Exp, scale=-0.5)
            for cc in range(4):
                k.op("dve", "scalar_tensor_tensor", [pss[cc].r, tmp.r, g.r], [dst.rs[cc]],
                     out=dst.t[:, cc, :], in0=pss[cc].t[:], scalar=g.t[:, gcol0 + cc:gcol0 + cc + 1], in1=tmp.t[:], op0=ALU.mult, op1=ALU.mult)

        aT_next = ns.emit(0, self.C_MIX + L * 16)
        for tb in range(NTB):
            aT = aT_next
            k.dma("sp", posi.t[:], bass.AP(self.pos.tensor, tb * TB, [[0, 64], [1, TB]]), [], [posi.r])
            k.op("dve", "tensor_copy", [posi.r], [ang.r], ang.t[:], posi.t[:])
            k.op("dve", "tensor_scalar", [ang.r, rc.r], [ang.r], ang.t[:], ang.t[:], rc.t[:, 0:1], None, op0=ALU.mult)
            k.op("dve", "tensor_scalar", [ang.r], [kf.r], kf.t[:], ang.t[:], 1.0 / TWO_PI, None, op0=ALU.mult)
            k.op("dve", "tensor_copy", [kf.r], [ki.r], ki.t[:], kf.t[:])
            k.op("dve", "tensor_copy", [ki.r], [kf.r], kf.t[:], ki.t[:])
            k.op("dve", "scalar_tensor_tensor", [kf.r, ang.r], [rr.r], out=rr.t[:], in0=kf.t[:], scalar=-C1, in1=ang.t[:], op0=ALU.mult, op1=ALU.add)
            k.op("dve", "scalar_tensor_tensor", [kf.r, rr.r], [rr.r], out=rr.t[:], in0=kf.t[:], scalar=-C2, in1=rr.t[:], op0=ALU.mult, op1=ALU.add)
            k.op("dve", "tensor_scalar", [rr.r], [rr.r], rr.t[:], rr.t[:], -PI_S, PI_S, op0=ALU.max, op1=ALU.min)
            k.op("act", "activation", [rr.r, rc.r], [sin2.r], out=sin2.t[:], in_=rr.t[:], func=AF.Sin, scale=rc.t[:, 1:2])
            k.op("dve", "tensor_scalar", [rr.r], [kf.r], kf.t[:], rr.t[:], HALF_PI, -TWO_PI, op0=ALU.is_gt, op1=ALU.mult)
            k.op("dve", "scalar_tensor_tensor", [rr.r, kf.r], [rr.r], out=rr.t[:], in0=rr.t[:], scalar=HALF_PI, in1=kf.t[:], op0=ALU.add, op1=ALU.add)
            k.op("dve", "tensor_scalar", [rr.r], [rr.r], rr.t[:], rr.t[:], -PI_S, PI_S, op0=ALU.max, op1=ALU.min)
            k.op("act", "activation", [rr.r], [cos2.r], out=cos2.t[:], in_=rr.t[:], func=AF.Sin)
            latent(aT, 0, 0, lat[0])
            latent(aT, 512, 4, lat[1])

            slots = {}

            def load_slot(name):
                slot = self.wslot()
                if name == 'C':
                    sv = slot.t[:, 0:16 * 64].rearrange("p (c f) -> p c f", c=16)
                    k.dma("pool", sv, w_in[:, :, 1024:1088], [], slot.rs)
                elif name[0] == 'Q':
                    half = int(name[1])
                    sv = slot.t[:, 0:4 * 1536].rearrange("p (c f) -> p c f", c=4)
                    k.dma("pool", sv, w_qb[:, :, half * 1536:(half + 1) * 1536], [], slot.rs)
                else:
                    half = int(name[2])
                    sv = slot.t[:].rearrange("p (c f) -> p c f", c=4)
                    k.dma("pool", sv, w_kvb[:, :, half * 2048:(half + 1) * 2048], [], slot.rs)
                slots[name] = (slot, sv)

            order = ['C', 'Q0', 'Q1', 'KV0', 'KV1']
            load_slot('C')
            load_slot('Q0')
            items = []
            it = dict(nrows=64, gcol=(g.t[0:64, 11:12], [g.r]), rope=True, dst=krst.t[:], dst_rs=[krst.r])
            it['mmf'] = (lambda ps, aT=aT: [(ps.t[0:64, :], slots['C'][1][:, kc, :], aT.t[:, kc, :], kc == 0, kc == 15) for kc in range(16)])
            it['readsf'] = (lambda aT=aT: slots['C'][0].rs + aT.rs)
            it['post'] = (lambda tb=tb: k.dma("sp", self.kropeT[:, tb * TB:(tb + 1) * TB], krst.t[:], [krst.r], [self.r_kr[tb]]))
            items.append(it)
            for half in range(2):
                nm = 'Q%d' % half
                nxt = order[order.index(nm) + 1]
                for part in range(2):
                    for hl in range(8):
                        h = half * 8 + hl
                        if part == 0:
                            it = dict(nrows=128, gcol=(g.t[:, 8:9], [g.r]), dst=qst.t[:, h, :], dst_rs=[qst.rs[h]])
                            it['mmf'] = (lambda ps, nm=nm, hl=hl: [(ps.t[:], slots[nm][1][:, kc, hl * 192:hl * 192 + 128], lat[0].t[:, kc, :], kc == 0, kc == 3)
                                                                  for kc in range(4)])
                            if hl == 0:
                                it['pre'] = (lambda nxt=nxt: load_slot(nxt))
                            if h == NH - 1:
                                it['post'] = (lambda tb=tb: k.dma("sp", self.qT[:, 0:128, tb * TB:(tb + 1) * TB].rearrange("h p t -> p h t"),
                                                                  qst.t[:], qst.rs, [self.r_q[tb]]))
                        else:
                            it = dict(nrows=64, gcol=(g.t[0:64, 9:10], [g.r]), rope=True, dst=qrst.t[:, h, :], dst_rs=[qrst.rs[h]])
                            it['mmf'] = (lambda ps, nm=nm, hl=hl: [(ps.t[0:64, :], slots[nm][1][:, kc, hl * 192 + 128:hl * 192 + 192], lat[0].t[:, kc, :], kc == 0, kc == 3)
                                                                  for kc in range(4)])
                            if h == NH - 1:
                                it['post'] = (lambda tb=tb: k.dma("sp", self.qT[:, 128:192, tb * TB:(tb + 1) * TB].rearrange("h p t -> p h t"),
                                                                  qrst.t[:], qrst.rs, [self.r_q[tb]]))
                        it['readsf'] = (lambda nm=nm: slots[nm][0].rs + lat[0].rs)
                        items.append(it)

            def v_part(half, tb=tb):
                slot, sv = slots['KV%d' % half]
                sv5 = slot.t[:].rearrange("p (c h w f) -> p c h w f", c=4, h=8, w=2)
                for ts in range(4):
                    for hg4 in range(2):
                        ps = self.bank()
                        k.mm(slot.rs + lat[1].rs, [ps.r],
                             [(ps.t[:].rearrange("p (h f) -> p h f", h=4), lat[1].t[:, kc, ts * 128:(ts + 1) * 128],
                               sv5[:, kc, hg4 * 4:(hg4 + 1) * 4, 1, :], kc == 0, kc == 3) for kc in range(4)])
                        vi = half * 2 + hg4
                        if (ts + hg4) % 2 == 0:
                            k.op("act", "copy", [ps.r], [vst.rs[ts * 4 + vi]], vst.t[:, ts, vi * 512:(vi + 1) * 512], ps.t[:])
                        else:
                            k.op("dve", "tensor_copy", [ps.r], [vst.rs[ts * 4 + vi]], vst.t[:, ts, vi * 512:(vi + 1) * 512], ps.t[:])
                if half == 1:
                    k.dma("sp", self.kT[:, :, tb * TB:(tb + 1) * TB].rearrange("h p t -> p h t"), kst.t[:], kst.rs, [self.r_k[tb]])
                    k.dma("sp", self.V[tb * TB:(tb + 1) * TB, :].rearrange("(s p) f -> p s f", p=128), vst.t[:], vst.rs, [self.r_v[tb]])

            for half in range(2):
                nm = 'KV%d' % half
                for hl in range(8):
                    h = half * 8 + hl
                    it = dict(nrows=128, gcol=(g.t[:, 10:11], [g.r]), dst=kst.t[:, h, :], dst_rs=[kst.rs[h]])
                    it['mmf'] = (lambda ps, nm=nm, hl=hl: [(ps.t[:], slots[nm][1][:, kc, hl * 256:hl * 256 + 128], lat[1].t[:, kc, :], kc == 0, kc == 3)
                                                          for kc in range(4)])
                    it['readsf'] = (lambda nm=nm: slots[nm][0].rs + lat[1].rs)
                    if hl == 0 and half == 0:
                        it['pre'] = (lambda: load_slot('KV1'))
                    items.append(it)
            if tb + 1 < NTB:
                aT_next = ns.emit(tb + 1, self.C_MIX + L * 16)
            self.norm_pipeline(items, sqh, rt, xr, t1, rope=(perm, cos2, sin2))
            v_part(0)
            v_part(1)

    def phase_sgu(self, es, L):
        k, nc = self.k, self.nc
        j = L // 4
        w_in = self.w['sgu_w_in'][j].rearrange("(c p) f -> p c f", p=128)
        w_out = self.w['sgu_w_out'][j].rearrange("(c p) d -> p c d", p=128)
        hTv = self.hT.rearrange("(c p) t -> p c t", p=128)
        ns = NormStream(self, es, "s_")
        rsd = Resid(self, es, "s_")
        self.wslots = [self.sb(es, "s_w%d" % i, [128, 8192], BF16, n=2) for i in range(2)]
        self.wslot_i = 0
        uT = self.sb(es, "s_uT", [128, 16, TB], BF16, n=16)
        vg = [self.sb(es, "s_vg%d" % i, [128, D], F32, n=4) for i in range(4)]
        junk = self.sb(es, "s_junk", [128, D], BF16)
        vvn = [self.sb(es, "s_vvn%d" % i, [128, D], BF16) for i in range(4)]
        ss = self.sb(es, "s_ss", [128, 4], F32, n=4)
        wsT = self.sb(es, "s_wsT", [128, 16, 128], BF16, n=16)
        bsb = self.sb(es, "s_bsb", [128, 16, 128], F32)
        vgb = self.sb(es, "s_vgb", [128, D], F32)
        ga = [self.sb(es, "s_ga%d" % i, [128, TB], F32) for i in range(3)]
        gb = [self.sb(es, "s_gb%d" % i, [128, TB], F32) for i in range(3)]
        m1 = [self.sb(es, "s_m1%d" % i, [128, TB], F32) for i in range(2)]
        wl = [self.sb(es, "s_wl%d" % i, [128, 128], F32) for i in range(2)]
        wt = [self.sb(es, "s_wt%d" % i, [128, 128], F32) for i in range(2)]
        for g_ in range(16):
            u = g_ % 2
            k.dma("sp", wl[u].t[:], self.w['sgu_w_s'][j, g_, :, :], [], [wl[u].r])
            ps = self.bank()
            k.mm([wl[u].r, self.ident.r], [ps.r], [(ps.t[:, 0:128], wl[u].t[:], self.ident.t[:])], transpose=True)
            k.op("dve", "tensor_copy", [ps.r], [wt[u].r], wt[u].t[:], ps.t[:, 0:128])
            k.op("pool", "affine_select", [wt[u].r], [wsT.rs[g_]], out=wsT.t[:, g_, :], in_=wt[u].t[:], pattern=[[1, 128]],
                 compare_op=ALU.is_ge, fill=0.0, base=0, channel_multiplier=-1)
        bs = self.w['sgu_b_s'][j]
        k.dma("sp", bsb.t[:].rearrange("p g t -> p (g t)"), bass.AP(bs.tensor, bs.offset, [[0, 128], [1, 16 * 128]]), [], [bsb.r])
        vgn = self.w['sgu_v_gain'][j]
        k.dma("sp", vgb.t[:], bass.AP(vgn.tensor, vgn.offset, [[0, 128], [1, D]]), [], [vgb.r])
        def gelu_pipeline(items):
            banks = self.banks
            for i, it in enumerate(items):
                it['ps'] = banks[i % 6]
                it['ga'] = ga[i % len(ga)]
                it['gb'] = gb[i % len(gb)]

            def g0(it):
                if it.get('pre'):
                    it['pre']()
                k.mm(it['readsf'](), [it['ps'].r], it['mmf'](it['ps']))

            def g1(it):
                k.op("act", "activation", [it['ps'].r], [it['ga'].r], out=it['ga'].t[:], in_=it['ps'].t[:], func=AF.Square)

            def g2(it):
                a_ = it['ga']
                k.op("dve", "tensor_scalar", [a_.r], [a_.r], a_.t[:], a_.t[:], 0.044715, 1.0, op0=ALU.mult, op1=ALU.add)
                k.op("dve", "tensor_tensor", [a_.r, it['ps'].r], [a_.r], a_.t[:], a_.t[:], it['ps'].t[:], ALU.mult)

            def g3(it):
                k.op("act", "activation", [it['ga'].r], [it['gb'].r], out=it['gb'].t[:], in_=it['ga'].t[:], func=AF.Sigmoid, scale=1.5957691216057308)

            def g4(it):
                k.op("dve", "tensor_tensor", [it['gb'].r, it['ps'].r], it['dst_rs'], it['dst'], it['gb'].t[:], it['ps'].t[:], ALU.mult)

            self.pipeline(items, [(0, g0), (1, g1), (2, g2), (3, g3), (4, g4)])

        aT_next = ns.emit(0, self.C_MIX + L * 16)
        for tb in range(NTB):
            aT = aT_next
            slots = {}

            def load_group(gi_):
                slot = self.wslot()
                sv = slot.t[:].rearrange("p (c f) -> p c f", c=16)
                k.dma("pool", sv, w_in[:, :, gi_ * 512:(gi_ + 1) * 512], [], slot.rs)
                slots[gi_] = (slot, sv)

            load_group(0)
            items = []
            for gi_ in range(8):
                for sub in range(4):
                    if gi_ < 4:
                        fch = gi_ * 4 + sub
                        it = dict(dst=uT.t[:, fch, :], dst_rs=[uT.rs[fch]])
                        it['mmf'] = (lambda ps, gi_=gi_, sub=sub, aT=aT: [(ps.t[:], slots[gi_][1][:, kc, sub * 128:(sub + 1) * 128], aT.t[:, kc, :], kc == 0, kc == 15)
                                                                        for kc in range(16)])
                    else:
                        vc, ts = gi_ - 4, sub
                        it = dict(dst=vg[ts].t[:, vc * 512:(vc + 1) * 512], dst_rs=[vg[ts].rs[vc]])
                        it['mmf'] = (lambda ps, gi_=gi_, ts=ts, aT=aT: [(ps.t[:], aT.t[:, kc, ts * 128:(ts + 1) * 128], slots[gi_][1][:, kc, :], kc == 0, kc == 15)
                                                                       for kc in range(16)])
                    it['readsf'] = (lambda gi_=gi_, aT=aT: slots[gi_][0].rs + aT.rs)
                    if sub == 0 and gi_ + 1 < 8:
                        it['pre'] = (lambda gi_=gi_: load_group(gi_ + 1))
                    items.append(it)
            gelu_pipeline(items)
            for ts in range(4):
                k.op("act", "activation", vg[ts].rs, [junk.r], out=junk.t[:], in_=vg[ts].t[:], func=AF.Square)
                k.op("dve", "reduce_sum", [junk.r], [ss.rs[ts]], ss.t[:, ts:ts + 1], junk.t[:], mybir.AxisListType.X)
                k.op("act", "activation", [ss.rs[ts], self.epsc.r], [ss.rs[ts]], out=ss.t[:, ts:ts + 1], in_=ss.t[:, ts:ts + 1], func=AF.Sqrt,
                     scale=1.0 / D, bias=self.epsc.t[:, 0:1])
                k.op("dve", "reciprocal", [ss.rs[ts]], [ss.rs[ts]], ss.t[:, ts:ts + 1], ss.t[:, ts:ts + 1])
                k.op("dve", "scalar_tensor_tensor", vg[ts].rs + [ss.rs[ts], vgb.r], [vvn[ts].r],
                     out=vvn[ts].t[:], in0=vg[ts].t[:], scalar=ss.t[:, ts:ts + 1], in1=vgb.t[:], op0=ALU.mult, op1=ALU.mult)
            for g_ in range(16):
                ps = self.bank()
                k.mm([wsT.rs[g_]] + [vvn[ts].r for ts in range(4)], [ps.r],
                     [(ps.t[:, ts * 128:(ts + 1) * 128], vvn[ts].t[:, g_ * 128:(g_ + 1) * 128], wsT.t[:, g_, :], True, True) for ts in range(4)])
                mm_ = m1[g_ % 2]
                for ts in range(4):
                    k.op("dve", "tensor_tensor", [ps.r, bsb.r], [mm_.r], mm_.t[:, ts * 128:(ts + 1) * 128], ps.t[:, ts * 128:(ts + 1) * 128],
                         bsb.t[:, g_, :], ALU.add)
                k.op("dve", "tensor_tensor", [mm_.r, uT.rs[g_]], [uT.rs[g_]], uT.t[:, g_, :], mm_.t[:], uT.t[:, g_, :], ALU.mult)
            if tb + 1 < NTB:
                aT_next = ns.emit(tb + 1, self.C_MIX + L * 16)
            for dg in range(4):
                slot = self.wslot()
                sv = slot.t[:].rearrange("p (c f) -> p c f", c=16)
                k.dma("pool", sv, w_out[:, :, dg * 512:(dg + 1) * 512], [], slot.rs)
                for dd in range(4):
                    dc = dg * 4 + dd
                    ps = self.bank()
                    rb = rsd.load(tb, dc)
                    k.mm(slot.rs + uT.rs, [ps.r],
                         [(ps.t[:], sv[:, c, dd * 128:(dd + 1) * 128], uT.t[:, c, :], c == 0, c == 15) for c in range(16)])
                    rsd.add_store(rb, tb, dc, ps)

    def phase_attn(self, es, L, kind):
        k, nc = self.k, self.nc
        Vv = self.V.rearrange("(c p) f -> p c f", p=128)
        qh = [[self.sb(es, "a_q%d%d" % (i, c), [128, T], BF16) for c in range(2)] for i in range(2)]
        kh = [[self.sb(es, "a_k%d%d" % (i, c), [128, T], BF16) for c in range(2)] for i in range(2)]
        vh = [[self.sb(es, "a_v%d%d" % (i, c), [128, 16, HD], BF16) for c in range(2)] for i in range(2)]
        if kind == 'fox':
            qa = [[self.sb(es, "a_qa%d%d" % (i, c), [6, T], BF16) for c in range(2)] for i in range(2)]
            ka = [[self.sb(es, "a_ka%d%d" % (i, c), [6, T], BF16) for c in range(2)] for i in range(2)]
        if kind == 'mla':
            qr = [[self.sb(es, "a_qr%d%d" % (i, c), [64, T], BF16) for c in range(2)] for i in range(2)]
            kr = self.sb(es, "a_kr", [64, T], BF16)
            k.dma("sp", kr.t[:], self.kropeT[:, :], self.r_kr, [kr.r])
        ao = [[self.sb(es, "a_ao%d%d" % (i, c), [128, T], BF16, n=NTB) for c in range(2)] for i in range(2)]
        NP = 8
        pT = [self.sb(es, "a_pT%d" % i, [128, TB], BF16) for i in range(NP)]
        acc = [[self.sb(es, "a_acc%d%d" % (c, i), [128, TB], F32) for i in range(2)] for c in range(2)]
        rden = [self.sb(es, "a_rden%d" % c, [128, TB], F32) for c in range(2)]
        ones_f = self.sb(es, "a_ones", [128, 128], F32)
        k.op("dve", "memset", [], [ones_f.r], ones_f.t[:], 1.0)
        banks, tri = self.banks, self.tri

        def load_pair(hp):
            s_ = hp % 2
            for c in range(2):
                h = 2 * hp + c
                k.dma("sp", qh[s_][c].t[:], self.qT[h, 0:128, :], self.r_q, [qh[s_][c].r])
                k.dma("sp", kh[s_][c].t[:], self.kT[h, :, :], self.r_k, [kh[s_][c].r])
                k.dma("sp", vh[s_][c].t[:], Vv[:, :, h * HD:(h + 1) * HD], self.r_v, [vh[s_][c].r])
                if kind == 'fox':
                    k.dma("sp", qa[s_][c].t[:], self.augQ[h, :, :], self.r_aug[0:6], [qa[s_][c].r])
                    k.dma("sp", ka[s_][c].t[:], self.augK[h, :, :], self.r_aug[6:12], [ka[s_][c].r])
                if kind == 'mla':
                    k.dma("sp", qr[s_][c].t[:], self.qT[h, 128:192, :], self.r_q, [qr[s_][c].r])

        tiles = []
        blk = 0
        for hp in range(NH // 2):
            for qb in range(NTB):
                nkc = 4 * qb + 4
                for kc in range(nkc):
                    for c in range(2):
                        jd = kc - 4 * qb
                        tiles.append(dict(hp=hp, s=hp % 2, c=c, qb=qb, kc=kc, jd=jd, c0=(128 * jd if jd > 0 else 0),
                                          first=(kc == 0), last=(kc == nkc - 1), t0=qb * TB, par=blk % 2,
                                          pair_first=(qb == 0 and kc == 0 and c == 0), pair_last=(qb == NTB - 1 and kc == nkc - 1)))
                blk += 1
        pstart = {}
        for i, tl in enumerate(tiles):
            pstart.setdefault(tl['hp'], i)
            tl['prefetch'] = (i == pstart[tl['hp']] + 14)
            tl['u'] = i % NP
            tl['z'] = banks[i % 2]
            tl['O'] = banks[2 + 2 * tl['c'] + tl['par']]
            tl['Dn'] = banks[6 + tl['c']]
            tl['acc'] = acc[tl['c']][tl['par']]

        def s0(tl):
            if tl['pair_first'] and tl['hp'] == 0:
                load_pair(0)
            if tl['prefetch'] and tl['hp'] + 1 < NH // 2:
                load_pair(tl['hp'] + 1)
            s_, c, c0, kc, t0, Z = tl['s'], tl['c'], tl['c0'], tl['kc'], tl['t0'], tl['z']
            q_, k_ = qh[s_][c], kh[s_][c]
            reads = [q_.r, k_.r]
            if kind == 'fox':
                mms = [(Z.t[:, c0:TB], k_.t[:, kc * 128:(kc + 1) * 128], q_.t[:, t0 + c0:t0 + TB], True, False),
                       (Z.t[:, c0:TB], ka[s_][c].t[:, kc * 128:(kc + 1) * 128], qa[s_][c].t[:, t0 + c0:t0 + TB], False, True)]
                reads += [qa[s_][c].r, ka[s_][c].r]
            else:
                mms = [(Z.t[:, c0:TB], k_.t[:, kc * 128:(kc + 1) * 128], q_.t[:, t0 + c0:t0 + TB], True, False),
                       (Z.t[:, c0:TB], kr.t[:, kc * 128:(kc + 1) * 128], qr[s_][c].t[:, t0 + c0:t0 + TB], False, True)]
                reads += [qr[s_][c].r, kr.r]
            k.mm(reads, [Z.r], mms)

        def s1(tl):
            c0, u, Z = tl['c0'], tl['u'], tl['z']
            k.op("act", "activation", [Z.r], [pT[u].r], out=pT[u].t[:, c0:TB], in_=Z.t[:, c0:TB], func=AF.Exp)

        def s2(tl):
            c0, u, a_ = tl['c0'], tl['u'], tl['acc']
            p = pT[u]
            if tl['jd'] >= 0:
                k.op("dve", "tensor_tensor", [p.r, tri.r], [p.r], p.t[:, c0:c0 + 128], p.t[:, c0:c0 + 128], tri.t[:, 0, :], ALU.mult)
            if tl['first']:
                k.op("dve", "tensor_copy", [p.r], [a_.r], a_.t[:], p.t[:])
            else:
                k.op("dve", "tensor_tensor", [p.r, a_.r], [a_.r], a_.t[:, c0:TB], a_.t[:, c0:TB], p.t[:, c0:TB], ALU.add)

        def s3(tl):
            s_, c, c0, u, kc, O = tl['s'], tl['c'], tl['c0'], tl['u'], tl['kc'], tl['O']
            v_ = vh[s_][c]
            k.mm([v_.r, pT[u].r], [O.r], [(O.t[:, c0:TB], v_.t[:, kc, :], pT[u].t[:, c0:TB], tl['first'], tl['last'])])

        def s4(tl):
            if tl['last']:
                Dn, a_ = tl['Dn'], tl['acc']
                k.mm([ones_f.r, a_.r], [Dn.r], [(Dn.t[:], ones_f.t[:], a_.t[:], True, True)])

        def s5(tl):
            if tl['last']:
                Dn, rd = tl['Dn'], rden[tl['c']]
                k.op("act", "activation", [Dn.r], [rd.r], out=rd.t[:], in_=Dn.t[:], func=AF.Ln)
                k.op("act", "activation", [rd.r], [rd.r], out=rd.t[:], in_=rd.t[:], func=AF.Exp, scale=-1.0)

        def s6(tl):
            if tl['last']:
                c, qb, t0, O, rd = tl['c'], tl['qb'], tl['t0'], tl['O'], rden[tl['c']]
                a_ = ao[tl['s']][c]
                k.op("dve", "tensor_tensor", [O.r, rd.r], [a_.rs[qb]], a_.t[:, t0:t0 + TB], O.t[:], rd.t[:], ALU.mult)
                if tl['pair_last']:
                    h = 2 * tl['hp'] + c
                    k.dma("sp", self.aoT[h * HD:(h + 1) * HD, :], a_.t[:], a_.rs, [self.r_ao[h]])

        self.pipeline(tiles, [(0, s0), (1, s1), (2, s2), (3, s3), (4, s4), (5, s5), (6, s6)])

    def pipeline(self, tiles, stages):
        n = len(tiles)
        maxoff = max(off for off, _ in stages)
        order = sorted(stages, key=lambda st: -st[0])
        for step in range(n + maxoff + 1):
            for off, fn in order:
                t = step - off
                if 0 <= t < n:
                    fn(tiles[t])

    def phase_attn_sb(self, es, L):
        k, nc = self.k, self.nc
        Vv = self.V.rearrange("(c p) f -> p c f", p=128)
        qh = [[self.sb(es, "b_q%d%d" % (i, c), [128, T], BF16) for c in range(2)] for i in range(2)]
        kh = [[self.sb(es, "b_k%d%d" % (i, c), [128, T], BF16) for c in range(2)] for i in range(2)]
        vh = [[self.sb(es, "b_v%d%d" % (i, c), [128, 16, HD], BF16) for c in range(2)] for i in range(2)]
        ao = [[self.sb(es, "b_ao%d%d" % (i, c), [128, T], BF16, n=NTB) for c in range(2)] for i in range(2)]
        NP = 12
        e_t = [self.sb(es, "b_e%d" % i, [128, TB], F32) for i in range(NP)]
        ln_t = [self.sb(es, "b_ln%d" % i, [128, TB], F32) for i in range(NP)]
        lf_t = [self.sb(es, "b_lf%d" % i, [128, TB], F32) for i in range(NP)]
        l1_t = [self.sb(es, "b_l1%d" % i, [128, TB], BF16) for i in range(NP)]
        l2_t = [self.sb(es, "b_l2%d" % i, [128, TB], BF16) for i in range(NP)]
        pT = [self.sb(es, "b_p%d" % i, [128, TB], BF16) for i in range(NP)]
        banks, tri, zbf = self.banks, self.tri, self.zbf
        X = [banks[3], banks[4]]
        O = [banks[5], banks[6]]

        def load_pair(hp):
            s_ = hp % 2
            for c in range(2):
                h = 2 * hp + c
                k.dma("sp", qh[s_][c].t[:], self.qT[h, 0:128, :], self.r_q, [qh[s_][c].r])
                k.dma("sp", kh[s_][c].t[:], self.kT[h, :, :], self.r_k, [kh[s_][c].r])
                k.dma("sp", vh[s_][c].t[:], Vv[:, :, h * HD:(h + 1) * HD], self.r_v, [vh[s_][c].r])

        tiles = []
        for hp in range(NH // 2):
            for qb in range(NTB):
                nkc = 4 * qb + 4
                for kc in range(nkc - 1, -1, -1):
                    for c in range(2):
                        jd = kc - 4 * qb
                        tiles.append(dict(hp=hp, s=hp % 2, c=c, qb=qb, kc=kc, jd=jd, c0=(128 * jd if jd > 0 else 0),
                                          first=(kc == nkc - 1), last=(kc == 0), t0=qb * TB,
                                          pair_first=(qb == 0 and kc == nkc - 1 and c == 0), pair_last=(qb == NTB - 1 and kc == 0)))
        pstart = {}
        for i, tl in enumerate(tiles):
            pstart.setdefault(tl['hp'], i)
            tl['prefetch'] = (i == pstart[tl['hp']] + 14)
            tl['u'] = i % NP
            tl['z'] = banks[i % 3]

        def s0(tl):
            if tl['pair_first'] and tl['hp'] == 0:
                load_pair(0)
            if tl['prefetch'] and tl['hp'] + 1 < NH // 2:
                load_pair(tl['hp'] + 1)
            q_, k_ = qh[tl['s']][tl['c']], kh[tl['s']][tl['c']]
            Z, c0, kc, t0 = tl['z'], tl['c0'], tl['kc'], tl['t0']
            k.mm([q_.r, k_.r], [Z.r], [(Z.t[:, c0:TB], k_.t[:, kc * 128:(kc + 1) * 128], q_.t[:, t0 + c0:t0 + TB], True, True)])

        def s1(tl):
            Z, c0, u = tl['z'], tl['c0'], tl['u']
            k.op("act", "activation", [Z.r], [e_t[u].r], out=e_t[u].t[:, c0:TB], in_=Z.t[:, c0:TB], func=AF.Exp, scale=-1.0)
            k.op("act", "activation", [e_t[u].r], [ln_t[u].r], out=ln_t[u].t[:, c0:TB], in_=e_t[u].t[:, c0:TB], func=AF.Ln, bias=1.0)

        def s2(tl):
            Z, c0, u = tl['z'], tl['c0'], tl['u']
            k.op("dve", "tensor_tensor", [Z.r, ln_t[u].r], [lf_t[u].r], lf_t[u].t[:, c0:TB], Z.t[:, c0:TB], ln_t[u].t[:, c0:TB], ALU.add)

        def s3(tl):
            c0, u = tl['c0'], tl['u']
            if tl['jd'] >= 0:
                k.op("pool", "tensor_tensor", [lf_t[u].r, tri.r], [lf_t[u].r], lf_t[u].t[:, c0:c0 + 128], lf_t[u].t[:, c0:c0 + 128],
                     tri.t[:, 3, :], ALU.mult)

        def s3b(tl):
            c0, u = tl['c0'], tl['u']
            k.op("dve", "tensor_copy", [lf_t[u].r], [l1_t[u].r], l1_t[u].t[:, c0:TB], lf_t[u].t[:, c0:TB])

        def s4(tl):
            c0, u = tl['c0'], tl['u']
            k.op("pool", "tensor_tensor", [lf_t[u].r, l1_t[u].r], [l2_t[u].r], l2_t[u].t[:, c0:TB], lf_t[u].t[:, c0:TB], l1_t[u].t[:, c0:TB], ALU.subtract)

        def s5(tl):
            c, c0, u = tl['c'], tl['c0'], tl['u']
            q_ = qh[tl['s']][c]
            if tl['first']:
                k.mm([zbf.r, q_.r], [X[c].r], [(X[c].t[:], zbf.t[:], q_.t[:, 0:TB], True, True)])
            k.mm([tri.r, l1_t[u].r, l2_t[u].r], [X[c].r],
                 [(X[c].t[:, c0:TB], tri.t[:, 1, :], l1_t[u].t[:, c0:TB], False, False, True),
                  (X[c].t[:, c0:TB], tri.t[:, 1, :], l2_t[u].t[:, c0:TB], False, True, True)])

        def s6(tl):
            c, c0, u = tl['c'], tl['c0'], tl['u']
            k.op("dve", "tensor_tensor", [X[c].r, ln_t[u].r], [e_t[u].r], e_t[u].t[:, c0:TB], X[c].t[:, c0:TB], ln_t[u].t[:, c0:TB], ALU.add)

        def s7(tl):
            c, c0, u = tl['c'], tl['c0'], tl['u']
            if not tl['last']:
                k.mm([tri.r, l1_t[u].r, l2_t[u].r], [X[c].r],
                     [(X[c].t[:, c0:TB], tri.t[:, 2, :], l1_t[u].t[:, c0:TB], False, False, True),
                      (X[c].t[:, c0:TB], tri.t[:, 2, :], l2_t[u].t[:, c0:TB], False, True, True)])
            k.op("act", "activation", [e_t[u].r], [pT[u].r], out=pT[u].t[:, c0:TB], in_=e_t[u].t[:, c0:TB], func=AF.Exp, scale=-1.0)

        def s8(tl):
            c0, u = tl['c0'], tl['u']
            if tl['jd'] >= 0:
                k.op("pool", "tensor_tensor", [pT[u].r, tri.r], [pT[u].r], pT[u].t[:, c0:c0 + 128], pT[u].t[:, c0:c0 + 128], tri.t[:, 3, :], ALU.mult)

        def s9(tl):
            c, c0, u, kc = tl['c'], tl['c0'], tl['u'], tl['kc']
            q_, v_ = qh[tl['s']][c], vh[tl['s']][c]
            if tl['first']:
                k.mm([zbf.r, q_.r], [O[c].r], [(O[c].t[:], zbf.t[:], q_.t[:, 0:TB], True, False)])
            k.mm([v_.r, pT[u].r], [O[c].r], [(O[c].t[:, c0:TB], v_.t[:, kc, :], pT[u].t[:, c0:TB], False, tl['last'])])

        def s10(tl):
            if tl['last']:
                c, qb, t0 = tl['c'], tl['qb'], tl['t0']
                a_ = ao[tl['s']][c]
                k.op("act", "copy", [O[c].r], [a_.rs[qb]], a_.t[:, t0:t0 + TB], O[c].t[:])
                if tl['pair_last']:
                    h = 2 * tl['hp'] + c
                    k.dma("sp", self.aoT[h * HD:(h + 1) * HD, :], a_.t[:], a_.rs, [self.r_ao[h]])

        self.pipeline(tiles, [(0, s0), (1, s1), (2, s2), (3, s3), (4, s3b), (5, s4), (6, s5), (7, s6), (8, s7), (9, s8), (10, s9), (11, s10)])

    def phase_oproj(self, es, w_out):
        k, nc = self.k, self.nc
        w = w_out.rearrange("(c p) d -> p c d", p=128)
        aov = self.aoT.rearrange("(c p) t -> p c t", p=128)
        rsd = Resid(self, es, "o_", n=6)
        ab = [self.sb(es, "o_ab%d" % i, [128, 16, TB], BF16) for i in range(2)]
        wres = self.sb(es, "o_wres", [128, 16, D], BF16, n=4)
        for dg in range(4):
            k.dma("pool", wres.t[:, :, dg * 512:(dg + 1) * 512], w[:, :, dg * 512:(dg + 1) * 512], [], [wres.rs[dg]])
        for tb in range(NTB):
            a = ab[tb % 2]
            k.dma("sp", a.t[:], aov[:, :, tb * TB:(tb + 1) * TB], self.r_ao, [a.r])
            for dc in range(16):
                ps = self.bank()
                rb = rsd.load(tb, dc)
                k.mm([wres.rs[dc // 4], a.r], [ps.r],
                     [(ps.t[:], wres.t[:, c, dc * 128:(dc + 1) * 128], a.t[:, c, :], c == 0, c == 15) for c in range(16)])
                rsd.add_store(rb, tb, dc, ps)

def make_consts():
    i = np.arange(128)
    tri = np.stack([
        (i[None, :] >= i[:, None]),
        (i[:, None] > i[None, :]),
        (i[:, None] <= i[None, :]),
        (i[None, :] > i[:, None]),
    ]).astype(np.float32)
    perm = np.zeros((64, 64), np.float32)
    for c in range(64):
        perm[(c + 32) % 64, c] = 1.0
    inv_freq = (np.float32(10000.0) ** (-(np.arange(0, 64, 2, dtype=np.float32)) / np.float32(64))).astype(np.float32)
    rope = np.zeros((64, 2), np.float32)
    rope[:, 0] = np.concatenate([inv_freq, inv_freq])
    rope[:32, 1] = -1.0
    rope[32:, 1] = 1.0
    return {"c_ident": np.eye(128, dtype=np.float32), "c_tri": tri, "c_perm": perm, "c_rope": rope}


def run(inputs, layers=None, n_cores=8, trace=False, debug=False):
    shapes = {n: inputs[n].shape for n in WEIGHT_NAMES}
    P = Prog(layers)
    P.debug = debug
    nc = P.build(shapes)
    consts = make_consts()
    in_maps = []
    for c in range(n_cores):
        m = {"x": np.ascontiguousarray(inputs['x'][c]),
             "positions": np.ascontiguousarray(inputs['positions'][c]).astype(np.int32)}
        for n in WEIGHT_NAMES:
            m[n] = np.ascontiguousarray(inputs[n], dtype=np.float32)
        m.update(consts)
        in_maps.append(m)
    if trace:
        res = run_bass_kernel_spmd(nc, in_maps, core_ids=list(range(n_cores)), trace=True)
    else:
        res = run_bass_kernel_spmd(nc, in_maps, core_ids=list(range(n_cores)))
    return np.stack([r["out"] for r in res.results], axis=0), res, P


def kernel(**inputs):
    inputs = {k_: np.asarray(v) for k_, v in inputs.items()}
    out, _, _ = run(inputs, None, 8)
    return out.astype(np.float32)
```

```python
import contextlib
import numpy as np
import ml_dtypes
import concourse.bass as bass
import concourse.mybir as mybir
from concourse.bass_utils import run_bass_kernel_spmd

F32 = mybir.dt.float32
BF16 = mybir.dt.bfloat16
I32 = mybir.dt.int32
AF = mybir.ActivationFunctionType
ALU = mybir.AluOpType

D = 2048
T = 2048
NH = 16
HD = 128
DFF = 5632
NFC = DFF // 128
EPS = 1e-6
TB = 512
NTB = T // TB
DEPTH = 4

WEIGHT_NAMES = [
    'mix_norm', 'ffn_norm',
    'fox_w_in', 'fox_b_f', 'fox_q_gain', 'fox_k_gain', 'fox_w_out',
    'mla_w_in', 'mla_q_a_gain', 'mla_kv_a_gain', 'mla_w_q_b', 'mla_w_kv_b',
    'mla_q_gain', 'mla_k_gain', 'mla_w_out',
    'sb_w_in', 'sb_q_gain', 'sb_k_gain', 'sb_w_out',
    'sgu_w_in', 'sgu_v_gain', 'sgu_w_s', 'sgu_b_s', 'sgu_w_out',
    'ffn_w_up', 'ffn_conv_w', 'ffn_conv_b', 'ffn_w_down',
]


class Res:
    __slots__ = ("w", "r")

    def __init__(self):
        self.w = None
        self.r = {}


def RL(n):
    return [Res() for _ in range(n)]


class K:
    def __init__(self, nc, es, same_engine_sync=True):
        self.nc = nc
        self.eng = {"pe": nc.tensor, "act": nc.scalar, "dve": nc.vector,
                    "pool": nc.gpsimd, "sp": nc.sync}
        self.sems = {}
        self.cnt = {}
        for e in ("pe", "act", "dve", "pool"):
            self.sems[e] = es.enter_context(nc.semaphore("s_" + e))
            self.cnt[e] = 0
        self.dq = {}
        for q, n in (("sp", 16), ("pool", 8), ("act", 4)):
            lst = []
            for i in range(n):
                key = "d_%s_%d" % (q, i)
                self.sems[key] = es.enter_context(nc.semaphore(key))
                self.cnt[key] = 0
                lst.append(key)
            self.dq[q] = [lst, 0]
        self.seen = {e: {} for e in self.eng}
        self.same = same_engine_sync
        self.ninst = 0

    def _wait(self, e, deps):
        seen = self.seen[e]
        best = {}
        for tok in deps:
            if tok is None:
                continue
            key, val = tok
            if key == e and (e == "pe" or not self.same):
                continue
            if seen.get(key, 0) >= val:
                continue
            if best.get(key, 0) < val:
                best[key] = val
        for key, val in best.items():
            self.eng[e].wait_ge(self.sems[key], val)
            seen[key] = val
            self.ninst += 1

    @staticmethod
    def _deps(reads, writes):
        deps = []
        for r in reads:
            deps.append(r.w)
        for w in writes:
            deps.append(w.w)
            for kk, v in w.r.items():
                deps.append((kk, v))
        return deps

    @staticmethod
    def _commit(tok, reads, writes):
        key, val = tok
        for r in reads:
            if r.r.get(key, 0) < val:
                r.r[key] = val
        for w in writes:
            w.w = tok
            w.r = {}

    def op(self, e, name, reads, writes, *a, **kw):
        self._wait(e, self._deps(reads, writes))
        ins = getattr(self.eng[e], name)(*a, **kw)
        self.cnt[e] += 1
        ins.then_inc(self.sems[e], 1)
        self.ninst += 1
        self._commit((e, self.cnt[e]), reads, writes)

    def mm(self, reads, writes, mms, transpose=False):
        e = "pe"
        self._wait(e, self._deps(reads, writes))
        ins = None
        for m in mms:
            if transpose:
                ins = self.nc.tensor.transpose(m[0], m[1], m[2])
            else:
                if len(m) > 5 and m[5]:
                    ins = self.nc.tensor.matmul(m[0], m[1], m[2], start=m[3], stop=m[4], skip_group_check=True)
                else:
                    ins = self.nc.tensor.matmul(m[0], m[1], m[2], start=m[3], stop=m[4])
            self.ninst += 1
        self.cnt[e] += 1
        ins.then_inc(self.sems[e], 1)
        self._commit((e, self.cnt[e]), reads, writes)

    def dma(self, q, out, in_, reads, writes, **kw):
        lst, idx = self.dq[q]
        key = lst[idx % len(lst)]
        self.dq[q][1] = idx + 1
        deps = self._deps(reads, writes)
        if self.cnt[key] > 0:
            deps.append((key, self.cnt[key]))
        self._wait(q, deps)
        self.cnt[key] += 16
        self.eng[q].dma_start(out=out, in_=in_, **kw).then_inc(self.sems[key], 16)
        self.ninst += 1
        self._commit((key, self.cnt[key]), reads, writes)

    def barrier(self):
        toks = [(key, c) for key, c in self.cnt.items() if c > 0]
        for e in self.eng:
            self._wait(e, [t for t in toks if not (t[0] == e and e == "pe")])


class Buf:
    def __init__(self, t, n=1):
        self.t = t
        self.rs = RL(n)

    @property
    def r(self):
        return self.rs[0]


class NormStream:
    def __init__(self, P, es, pfx):
        self.P = P
        self.aT = [P.sb(es, pfx + "aT%d" % i, [128, 16, TB], BF16, n=16) for i in range(2)]
        self.hc = [P.sb(es, pfx + "hc%d" % i, [128, TB], F32) for i in range(6)]
        self.sq = [P.sb(es, pfx + "sq%d" % i, [128, TB], BF16) for i in range(4)]
        self.rstd = [P.sb(es, pfx + "rs%d" % i, [128, TB], F32) for i in range(2)]
        self.i = 0
        self.n = 0

    def emit(self, tb, gcol0):
        P, k = self.P, self.P.k
        aT = self.aT[self.n % 2]
        rstd = self.rstd[self.n % 2]
        self.n += 1
        hTv = P.hT.rearrange("(c p) t -> p c t", p=128)
        ps = P.banks[7]
        for c in range(16):
            hc = self.hc[self.i % 6]
            sq = self.sq[self.i % 4]
            self.i += 1
            k.dma("sp", hc.t[:], hTv[:, c, tb * TB:(tb + 1) * TB], [P.r_hT[tb][c]], [hc.r])
            k.op("act", "activation", [hc.r], [sq.r], out=sq.t[:], in_=hc.t[:], func=AF.Square)
            k.mm([P.ones_bf.r, sq.r], [ps.r], [(ps.t[:], P.ones_bf.t[:], sq.t[:], c == 0, c == 15)])
        k.op("act", "activation", [ps.r, P.epsc.r], [rstd.r], out=rstd.t[:], in_=ps.t[:], func=AF.Ln, scale=1.0 / D, bias=P.epsc.t[:, 0:1])
        k.op("act", "activation", [rstd.r], [rstd.r], out=rstd.t[:], in_=rstd.t[:], func=AF.Exp, scale=-0.5)
        for c in range(16):
            hc = self.hc[self.i % 6]
            self.i += 1
            k.dma("sp", hc.t[:], hTv[:, c, tb * TB:(tb + 1) * TB], [P.r_hT[tb][c]], [hc.r])
            k.op("dve", "scalar_tensor_tensor", [hc.r, rstd.r, P.cols.r], [aT.rs[c]],
                 out=aT.t[:, c, :], in0=hc.t[:], scalar=P.cols.t[:, gcol0 + c:gcol0 + c + 1], in1=rstd.t[:], op0=ALU.mult, op1=ALU.mult)
        return aT


class Resid:
    def __init__(self, P, es, pfx, n=4):
        self.P = P
        self.ring = [P.sb(es, pfx + "rb%d" % i, [128, TB], F32) for i in range(n)]
        self.i = 0

    def load(self, tb, dc):
        P, k = self.P, self.P.k
        hTv = P.hT.rearrange("(c p) t -> p c t", p=128)
        rb = self.ring[self.i % len(self.ring)]
        self.i += 1
        k.dma("sp", rb.t[:], hTv[:, dc, tb * TB:(tb + 1) * TB], [P.r_hT[tb][dc]], [rb.r])
        return rb

    def add_store(self, rb, tb, dc, ps):
        P, k = self.P, self.P.k
        hTv = P.hT.rearrange("(c p) t -> p c t", p=128)
        k.op("dve", "tensor_tensor", [ps.r, rb.r], [rb.r], rb.t[:], rb.t[:], ps.t[:], ALU.add)
        k.dma("sp", hTv[:, dc, tb * TB:(tb + 1) * TB], rb.t[:], [rb.r], [P.r_hT[tb][dc]])

    def add_only(self, rb, ps):
        k = self.P.k
        k.op("dve", "tensor_tensor", [ps.r, rb.r], [rb.r], rb.t[:], rb.t[:], ps.t[:], ALU.add)

    def emit_out(self, rb, tb, dc, obuf):
        P, k = self.P, self.P.k
        ps = P.bank()
        k.mm([rb.r, P.ident.r], [ps.r],
             [(ps.t[:, ts * 128:(ts + 1) * 128], rb.t[:, ts * 128:(ts + 1) * 128], P.ident.t[:]) for ts in range(4)], transpose=True)
        k.op("act", "copy", [ps.r], [obuf.r], obuf.t[:], ps.t[:])
        r = Res()
        k.dma("sp", P.out[tb * TB:(tb + 1) * TB, dc * 128:(dc + 1) * 128].rearrange("(s p) d -> p s d", p=128),
              obuf.t[:].rearrange("p (s d) -> p s d", s=4), [obuf.r], [r])


class Prog:
    def __init__(self, layers, do_in=True, do_out=True, debug_out=None):
        self.layers = layers
        self.fuse_out = (layers is None) or ((DEPTH - 1) in layers and 'ffn' in layers[DEPTH - 1])
        self.nc = nc = bass.Bass("TRN2", target_bir_lowering=False)
        self.din = {}
        self.w = {}

    def declare(self, shapes):
        nc = self.nc
        self.x = nc.dram_tensor("x", [T, D], F32, kind="ExternalInput").ap()
        self.pos = nc.dram_tensor("positions", [T], I32, kind="ExternalInput").ap()
        for name in WEIGHT_NAMES:
            self.w[name] = nc.dram_tensor(name, list(shapes[name]), F32, kind="ExternalInput").ap()
        for name in ("c_ident", "c_consts"):
            pass
        self.c_ident = nc.dram_tensor("c_ident", [128, 128], F32, kind="ExternalInput").ap()
        self.c_tri = nc.dram_tensor("c_tri", [4, 128, 128], F32, kind="ExternalInput").ap()
        dbg = {"kind": "ExternalOutput"} if getattr(self, "debug", False) else {}
        self.qT = nc.dram_tensor("qT_scr", [NH, 192, T], BF16, **dbg).ap()
        self.kT = nc.dram_tensor("kT_scr", [NH, 128, T], BF16, **dbg).ap()
        self.V = nc.dram_tensor("V_scr", [T, NH * HD], BF16, **dbg).ap()
        self.aoT = nc.dram_tensor("aoT_scr", [NH * HD, T], BF16).ap()
        self.augQ = nc.dram_tensor("augQ_scr", [NH, 6, T], BF16).ap()
        self.augK = nc.dram_tensor("augK_scr", [NH, 6, T], BF16).ap()
        self.r_q, self.r_k, self.r_v, self.r_ao = RL(NTB), RL(NTB), RL(NTB), RL(NH)
        self.r_aug = RL(12)
        self.kropeT = nc.dram_tensor("krope_scr", [64, T], BF16, **dbg).ap()
        self.r_kr = RL(NTB)
        self.c_perm = nc.dram_tensor("c_perm", [64, 64], F32, kind="ExternalInput").ap()
        self.c_rope = nc.dram_tensor("c_rope", [64, 2], F32, kind="ExternalInput").ap()
        self.out = nc.dram_tensor("out", [T, D], F32, kind="ExternalOutput").ap()
        self.hT = nc.dram_tensor("hT_scr", [D, T], F32).ap()
        self.r_hT = [RL(16) for _ in range(NTB)]

    def sb(self, es, name, shape, dt=F32, n=1):
        self.uid = getattr(self, "uid", 0) + 1
        t = es.enter_context(self.nc.sbuf_tensor("%s_%d" % (name, self.uid), shape, dt))
        return Buf(t, n)

    def bank(self):
        b = self.banks[self.bank_i % 7]
        self.bank_i += 1
        return b

    def wslot(self):
        s = self.wslots[self.wslot_i % len(self.wslots)]
        self.wslot_i += 1
        return s

    def build(self, shapes):
        nc = self.nc
        self.declare(shapes)
        with contextlib.ExitStack() as es:
            self.k = k = K(nc, es)
            self.banks = [Buf(es.enter_context(nc.psum_tensor("bank%d" % i, [128, 512], F32))) for i in range(8)]
            self.bank_i = 0
            self.ident = self.sb(es, "ident", [128, 128], F32)
            k.dma("sp", self.ident.t[:], self.c_ident[:, :], [], [self.ident.r])
            self.ones_bf = self.sb(es, "ones_bf", [128, 128], BF16)
            k.op("dve", "memset", [], [self.ones_bf.r], self.ones_bf.t[:], 1.0)
            self.zbf = self.sb(es, "zbf", [128, 128], BF16)
            k.op("dve", "memset", [], [self.zbf.r], self.zbf.t[:], 0.0)
            self.epsc = self.sb(es, "epsc", [128, 1], F32)
            k.op("dve", "memset", [], [self.epsc.r], self.epsc.t[:], EPS)
            self.tri = self.sb(es, "tri", [128, 4, 128], BF16)
            k.dma("pool", self.tri.t[:], self.c_tri.rearrange("a p f -> p a f"), [], [self.tri.r])
            self.cols = self.sb(es, "cols", [128, 2 * 64 + 4 * 264 + 352], F32)
            self.C_MIX, self.C_FFN, self.C_CW, self.C_CB = 0, 64, 128, 128 + 4 * 264
            with contextlib.ExitStack() as pes:
                stg = self.sb(pes, "stg", [128, 128], F32)
                self.load_cols(stg, self.w['mix_norm'].rearrange("l (c p) -> (l c) p", p=128), 64, self.C_MIX)
                self.load_cols(stg, self.w['ffn_norm'].rearrange("l (c p) -> (l c) p", p=128), 64, self.C_FFN)
                self.load_cols(stg, self.w['ffn_conv_w'].rearrange("l t (c p) -> (l t c) p", p=128), 4 * 264, self.C_CW)
                self.load_cols(stg, self.w['ffn_conv_b'].rearrange("l (c p) -> (l c) p", p=128), 352, self.C_CB)
                k.barrier()
            if self.layers is None or self.layers.get('in', True):
                with contextlib.ExitStack() as pes:
                    self.phase_in(pes)
                    k.barrier()
            for L in range(DEPTH):
                cfg = None if self.layers is None else self.layers.get(L)
                if self.layers is not None and cfg is None:
                    continue
                if cfg is None or 'mix' in cfg:
                    m = L % 4
                    if m == 3:
                        with contextlib.ExitStack() as pes:
                            self.phase_sgu(pes, L)
                            k.barrier()
                    else:
                        kind = ('fox', 'mla', 'sb')[m]
                        with contextlib.ExitStack() as pes:
                            if kind == 'mla':
                                self.phase_mla_proj(pes, L)
                            else:
                                self.phase_qkv(pes, L, kind)
                            k.barrier()
                        with contextlib.ExitStack() as pes:
                            if kind == 'sb':
                                self.phase_attn_sb(pes, L)
                            else:
                                self.phase_attn(pes, L, kind)
                            k.barrier()
                        with contextlib.ExitStack() as pes:
                            self.phase_oproj(pes, self.w[kind + '_w_out'][L // 4])
                            k.barrier()
                if cfg is None or 'ffn' in cfg:
                    with contextlib.ExitStack() as pes:
                        self.phase_ffn(pes, L)
                        k.barrier()
            if (self.layers is None or self.layers.get('out', True)) and not self.fuse_out:
                with contextlib.ExitStack() as pes:
                    self.phase_out(pes)
                    k.barrier()
        return nc

    def load_cols(self, stg, src, n, col0):
        k, nc = self.k, self.nc
        i = 0
        while i < n:
            m = min(128, n - i)
            k.dma("sp", stg.t[0:m, :], src[i:i + m, :], [], [stg.r])
            ps = self.bank()
            k.mm([stg.r, self.ident.r], [ps.r], [(ps.t[:, 0:m], stg.t[0:m, :], self.ident.t[0:m, 0:m])], transpose=True)
            k.op("dve", "tensor_copy", [ps.r], [self.cols.r], self.cols.t[:, col0 + i:col0 + i + m], ps.t[:, 0:m])
            i += m

    def rmsnorm(self, hb, gcol0, aT, sq, tmp):
        k, nc = self.k, self.nc
        ps = self.bank()
        for g4 in range(4):
            for c in range(4 * g4, 4 * g4 + 4):
                k.op("act", "activation", [hb.rs[c]], [sq.rs[c % 4]], out=sq.t[:, c % 4, :], in_=hb.t[:, c, :], func=AF.Square)
            k.mm([self.ones_bf.r] + sq.rs, [ps.r],
                 [(ps.t[:], self.ones_bf.t[:], sq.t[:, c % 4, :], c == 0, c == 15) for c in range(4 * g4, 4 * g4 + 4)])
        k.op("act", "activation", [ps.r, self.epsc.r], [tmp.r], out=tmp.t[:], in_=ps.t[:], func=AF.Sqrt, scale=1.0 / D, bias=self.epsc.t[:, 0:1])
        k.op("dve", "reciprocal", [tmp.r], [tmp.r], tmp.t[:], tmp.t[:])
        for c in range(16):
            k.op("dve", "scalar_tensor_tensor", [hb.rs[c], tmp.r, self.cols.r], [aT.rs[c]],
                 out=aT.t[:, c, :], in0=hb.t[:, c, :], scalar=self.cols.t[:, gcol0 + c:gcol0 + c + 1], in1=tmp.t[:],
                 op0=ALU.mult, op1=ALU.mult)

    def phase_in(self, es):
        k, nc = self.k, self.nc
        xin = [self.sb(es, "xin%d" % i, [128, D], F32) for i in range(2)]
        hb = [self.sb(es, "hin%d" % i, [128, 16, TB], F32, n=16) for i in range(2)]
        hTv = self.hT.rearrange("(c p) t -> p c t", p=128)
        for tb in range(NTB):
            h = hb[tb % 2]
            for ts in range(4):
                xi = xin[(tb * 4 + ts) % 2]
                t0 = tb * TB + ts * 128
                k.dma("sp" if ts % 2 == 0 else "act", xi.t[:], self.x[t0:t0 + 128, :], [], [xi.r])
                for dc4 in range(4):
                    ps = self.bank()
                    k.mm([xi.r, self.ident.r], [ps.r],
                         [(ps.t[:, j * 128:(j + 1) * 128], xi.t[:, (dc4 * 4 + j) * 128:(dc4 * 4 + j + 1) * 128], self.ident.t[:]) for j in range(4)],
                         transpose=True)
                    eng = "dve" if dc4 % 2 == 0 else "act"
                    rs = [h.rs[dc4 * 4 + j] for j in range(4)]
                    if eng == "dve":
                        k.op("dve", "tensor_copy", [ps.r], rs, h.t[:, dc4 * 4:dc4 * 4 + 4, ts * 128:(ts + 1) * 128],
                             ps.t[:].rearrange("p (j t) -> p j t", j=4))
                    else:
                        k.op("act", "copy", [ps.r], rs, h.t[:, dc4 * 4:dc4 * 4 + 4, ts * 128:(ts + 1) * 128],
                             ps.t[:].rearrange("p (j t) -> p j t", j=4))
            k.dma("pool", hTv[:, :, tb * TB:(tb + 1) * TB], h.t[:], h.rs, self.r_hT[tb])

    def phase_out(self, es):
        k, nc = self.k, self.nc
        hb = [self.sb(es, "hout%d" % i, [128, 16, TB], F32) for i in range(2)]
        ob = [self.sb(es, "oout%d" % i, [128, D], F32, n=4) for i in range(2)]
        hTv = self.hT.rearrange("(c p) t -> p c t", p=128)
        ro = Res()
        self.r_out = []
        for tb in range(NTB):
            h = hb[tb % 2]
            k.dma("sp", h.t[:], hTv[:, :, tb * TB:(tb + 1) * TB], self.r_hT[tb], [h.r])
            for ts in range(4):
                o = ob[(tb * 4 + ts) % 2]
                for dc4 in range(4):
                    ps = self.bank()
                    k.mm([h.r, self.ident.r], [ps.r],
                         [(ps.t[:, j * 128:(j + 1) * 128], h.t[:, dc4 * 4 + j, ts * 128:(ts + 1) * 128], self.ident.t[:]) for j in range(4)],
                         transpose=True)
                    if dc4 % 2 == 0:
                        k.op("dve", "tensor_copy", [ps.r], [o.rs[dc4]], o.t[:, dc4 * 512:(dc4 + 1) * 512], ps.t[:])
                    else:
                        k.op("act", "copy", [ps.r], [o.rs[dc4]], o.t[:, dc4 * 512:(dc4 + 1) * 512], ps.t[:])
                t0 = tb * TB + ts * 128
                r = Res()
                k.dma("sp", self.out[t0:t0 + 128, :], o.t[:], o.rs, [r])
                self.r_out.append(r)

    def phase_ffn(self, es, L):
        k, nc = self.k, self.nc
        w_up = self.w['ffn_w_up'][L].rearrange("(c p) f -> p c f", p=128)
        w_dn = self.w['ffn_w_down'][L].rearrange("(c p) d -> p c d", p=128)
        hTv = self.hT.rearrange("(c p) t -> p c t", p=128)
        ns = NormStream(self, es, "f_")
        final = (L == DEPTH - 1) and self.fuse_out
        rsd = Resid(self, es, "f_", n=4)
        pend, oi = [], [0]
        obufs = [self.sb(es, "f_ob%d" % i, [128, TB], F32) for i in range(2)] if final else []
        gT = self.sb(es, "f_gT", [128, NFC, TB], BF16, n=NFC)
        self.wslots = [self.sb(es, "f_w%d" % i, [128, NFC * 256], BF16, n=2) for i in range(3)]
        self.wslot_i = 0
        halo = self.sb(es, "f_halo", [128, 2 * NFC, 2], F32, n=2 * NFC)
        U = [[self.sb(es, "f_U%d%d" % (i, j), [128, TB + 2], F32, n=2) for j in range(2)] for i in range(2)]
        Y = [[self.sb(es, "f_Y%d%d" % (i, j), [128, TB], F32) for j in range(2)] for i in range(2)]
        SG = [self.sb(es, "f_S%d" % i, [128, TB], F32) for i in range(2)]
        cw0 = self.C_CW + L * 264
        cb0 = self.C_CB + L * 88
        cols = self.cols
        pair_i = 0
        aT_next = ns.emit(0, self.C_FFN + L * 16)
        for tb in range(NTB):
            aT = aT_next
            for pg in range(NFC // 2):
                slot = self.wslot()
                sv = slot.t[:, 0:8192].rearrange("p (c j f) -> p c j f", c=16, j=2)
                k.dma("pool", sv[:, :, 0, :], w_up[:, :, pg * 256:(pg + 1) * 256], [], [slot.rs[0]])
                k.dma("pool", sv[:, :, 1, :], w_up[:, :, DFF + pg * 256:DFF + (pg + 1) * 256], [], [slot.rs[1]])
                for j in range(2):
                    fc = pg * 2 + j
                    pss = [self.bank(), self.bank()]
                    for gv in range(2):
                        k.mm([slot.rs[gv]] + aT.rs, [pss[gv].r],
                             [(pss[gv].t[:], sv[:, kc, gv, j * 128:(j + 1) * 128], aT.t[:, kc, :], kc == 0, kc == 15) for kc in range(16)])
                    ys = []
                    for gv in range(2):
                        ch = gv * NFC + fc
                        u = U[pair_i % 2][gv]
                        y = Y[pair_i % 2][gv]
                        k.op("act", "copy", [pss[gv].r], [u.rs[0]], u.t[:, 2:TB + 2], pss[gv].t[:])
                        if tb == 0:
                            k.op("dve", "memset", [], [u.rs[1]], u.t[:, 0:2], 0.0)
                        else:
                            k.op("dve", "tensor_copy", [halo.rs[ch]], [u.rs[1]], u.t[:, 0:2], halo.t[:, ch, :])
                        k.op("act", "activation", [u.rs[0], cols.r], [y.r], out=y.t[:], in_=u.t[:, 2:TB + 2], func=AF.Identity,
                             scale=cols.t[:, cw0 + 2 * 88 + ch:cw0 + 2 * 88 + ch + 1], bias=cols.t[:, cb0 + ch:cb0 + ch + 1])
                        k.op("dve", "scalar_tensor_tensor", [u.rs[0], u.rs[1], cols.r, y.r], [y.r],
                             out=y.t[:], in0=u.t[:, 1:TB + 1], scalar=cols.t[:, cw0 + 88 + ch:cw0 + 88 + ch + 1], in1=y.t[:],
                             op0=ALU.mult, op1=ALU.add)
                        k.op("dve", "scalar_tensor_tensor", [u.rs[0], u.rs[1], cols.r, y.r], [y.r],
                             out=y.t[:], in0=u.t[:, 0:TB], scalar=cols.t[:, cw0 + ch:cw0 + ch + 1], in1=y.t[:],
                             op0=ALU.mult, op1=ALU.add)
                        if tb < NTB - 1:
                            k.op("dve", "tensor_copy", [u.rs[0]], [halo.rs[ch]], halo.t[:, ch, :], u.t[:, TB:TB + 2])
                        ys.append(y)
                    sg = SG[pair_i % 2]
                    k.op("act", "activation", [ys[0].r], [sg.r], out=sg.t[:], in_=ys[0].t[:], func=AF.Silu)
                    k.op("dve", "tensor_tensor", [sg.r, ys[1].r], [gT.rs[fc]], gT.t[:, fc, :], sg.t[:], ys[1].t[:], ALU.mult)
                    pair_i += 1
            if tb + 1 < NTB:
                aT_next = ns.emit(tb + 1, self.C_FFN + L * 16)
            for dg in range(8):
                slot = self.wslot()
                sv = slot.t[:].rearrange("p (c d) -> p c d", c=NFC)
                k.dma("pool", sv, w_dn[:, :, dg * 256:(dg + 1) * 256], [], [slot.rs[0], slot.rs[1]])
                for dd in range(2):
                    dc = dg * 2 + dd
                    ps = self.bank()
                    rb = rsd.load(tb, dc)
                    k.mm(slot.rs + gT.rs, [ps.r],
                         [(ps.t[:], sv[:, c, dd * 128:(dd + 1) * 128], gT.t[:, c, :], c == 0, c == NFC - 1) for c in range(NFC)])
                    if not final:
                        rsd.add_store(rb, tb, dc, ps)
                    else:
                        if pend:
                            rsd.emit_out(*pend.pop())
                        rsd.add_only(rb, ps)
                        pend.append((rb, tb, dc, obufs[oi[0] % len(obufs)]))
                        oi[0] += 1
            if final and pend:
                rsd.emit_out(*pend.pop())

    def qk_norm(self, ps, gcol, out_ap, out_rs, sqh, rt, nrows=128):
        k = self.k
        k.op("act", "activation", [ps.r], [sqh.r], out=sqh.t[0:nrows, :], in_=ps.t[0:nrows, :], func=AF.Square)
        ps2 = self.bank()
        k.mm([self.ones_bf.r, sqh.r], [ps2.r], [(ps2.t[0:nrows, :], self.ones_bf.t[0:nrows, 0:nrows], sqh.t[0:nrows, :], True, True)])
        k.op("act", "activation", [ps2.r, self.epsc.r], [rt.r], out=rt.t[0:nrows, :], in_=ps2.t[0:nrows, :], func=AF.Sqrt,
             scale=1.0 / nrows, bias=self.epsc.t[0:nrows, 0:1])
        k.op("dve", "reciprocal", [rt.r], [rt.r], rt.t[0:nrows, :], rt.t[0:nrows, :])
        k.op("dve", "scalar_tensor_tensor", [ps.r, rt.r] + gcol[1], out_rs,
             out=out_ap, in0=ps.t[0:nrows, :], scalar=gcol[0], in1=rt.t[0:nrows, :], op0=ALU.mult, op1=ALU.mult)


    def norm_pipeline(self, items, sqh, rt, xr=None, t1=None, rope=None):
        k, banks = self.k, self.banks
        for i, it in enumerate(items):
            it['ps'] = banks[i % 4]
            it['ps2'] = banks[4 + i % 2]
            it['psc'] = banks[6]
            it['sqh'] = sqh[i % len(sqh)]
            it['rt'] = rt[i % len(rt)]
            if it.get('rope'):
                it['xr'] = xr[i % len(xr)]
                it['t1'] = t1[i % len(t1)]

        def s0(it):
            if it.get('pre'):
                it['pre']()
            k.mm(it['readsf'](), [it['ps'].r], it['mmf'](it['ps']))

        def s1(it):
            n = it['nrows']
            k.op("act", "activation", [it['ps'].r], [it['sqh'].r], out=it['sqh'].t[0:n, :], in_=it['ps'].t[0:n, :], func=AF.Square)

        def s2(it):
            n = it['nrows']
            k.mm([self.ones_bf.r, it['sqh'].r], [it['ps2'].r], [(it['ps2'].t[0:n, :], self.ones_bf.t[0:n, 0:n], it['sqh'].t[0:n, :], True, True)])

        def s3(it):
            n = it['nrows']
            r_ = it['rt']
            k.op("act", "activation", [it['ps2'].r, self.epsc.r], [r_.r], out=r_.t[0:n, :], in_=it['ps2'].t[0:n, :], func=AF.Ln,
                 scale=1.0 / n, bias=self.epsc.t[0:n, 0:1])
            k.op("act", "activation", [r_.r], [r_.r], out=r_.t[0:n, :], in_=r_.t[0:n, :], func=AF.Exp, scale=-0.5)
            if it.get('rope'):
                dst, dst_rs = it['xr'].t[:], [it['xr'].r]
            else:
                dst, dst_rs = it['dst'], it['dst_rs']
            k.op("dve", "scalar_tensor_tensor", [it['ps'].r, r_.r] + it['gcol'][1], dst_rs,
                 out=dst, in0=it['ps'].t[0:n, :], scalar=it['gcol'][0], in1=r_.t[0:n, :], op0=ALU.mult, op1=ALU.mult)
            if not it.get('rope') and it.get('post'):
                it['post']()

        def s4(it):
            if it.get('rope'):
                perm, cos2, sin2 = rope
                x = it['xr']
                k.mm([perm.r, x.r], [it['psc'].r], [(it['psc'].t[0:64, :], perm.t[:], x.t[:], True, True)])
                k.op("dve", "tensor_tensor", [x.r, cos2.r], [it['t1'].r], it['t1'].t[:], x.t[:], cos2.t[:], ALU.mult)

        def s5(it):
            if it.get('rope'):
                perm, cos2, sin2 = rope
                x = it['xr']
                k.op("dve", "tensor_tensor", [it['psc'].r, sin2.r], [x.r], x.t[:], it['psc'].t[0:64, :], sin2.t[:], ALU.mult)
                k.op("dve", "tensor_tensor", [x.r, it['t1'].r], it['dst_rs'], it['dst'], x.t[:], it['t1'].t[:], ALU.add)
                if it.get('post'):
                    it['post']()

        self.pipeline(items, [(0, s0), (1, s1), (2, s2), (3, s3), (4, s4), (5, s5)])

    def phase_qkv(self, es, L, kind):
        k, nc = self.k, self.nc
        j = L // 4
        w_in = self.w[kind + '_w_in'][j].rearrange("(c p) f -> p c f", p=128)
        hTv = self.hT.rearrange("(c p) t -> p c t", p=128)
        gq = self.sb(es, "q_gq", [128, 2], F32)
        k.dma("sp", gq.t[:, 0:1], self.w[kind + '_q_gain'][j].rearrange("(p o) -> p o", o=1), [], [gq.r])
        k.dma("sp", gq.t[:, 1:2], self.w[kind + '_k_gain'][j].rearrange("(p o) -> p o", o=1), [], [gq.r])
        k.op("dve", "tensor_scalar", [gq.r], [gq.r], gq.t[:, 0:1], gq.t[:, 0:1], float(HD) ** -0.5, None, op0=ALU.mult)
        if kind == 'fox':
            nbf = self.sb(es, "q_nbf", [16, 1], F32)
            k.dma("sp", nbf.t[:], self.w['fox_b_f'][j].rearrange("(p o) -> p o", o=1), [], [nbf.r])
            k.op("dve", "tensor_scalar", [nbf.r], [nbf.r], nbf.t[:], nbf.t[:], -1.0, None, op0=ALU.mult)
            cumN = self.sb(es, "q_cum", [16, T], F32, n=NTB)
            ones16 = self.sb(es, "q_ones16", [16, TB], F32)
            k.op("dve", "memset", [], [ones16.r], ones16.t[:], 1.0)
            fe = self.sb(es, "q_fe", [16, TB], F32)
            wf = self.sb(es, "q_wf", [128, 16, 16], BF16)
            k.dma("pool", wf.t[:], w_in[:, :, 3 * NH * HD:3 * NH * HD + 16], [], [wf.r])
        with contextlib.ExitStack() as es2:
            ns = NormStream(self, es2, "q_")
            self.wslots = [self.sb(es2, "q_w%d" % i, [128, 8192], BF16, n=2) for i in range(3)]
            self.wslot_i = 0
            sqh = [self.sb(es2, "q_sqh%d" % i, [128, TB], BF16) for i in range(3)]
            rt = [self.sb(es2, "q_rt%d" % i, [128, TB], F32) for i in range(2)]
            stage = [self.sb(es2, "q_st%d" % i, [128, NH, TB], BF16, n=NH) for i in range(2)]
            vst = self.sb(es2, "q_vst", [128, 4, NH * HD], BF16, n=16)
            qi = 0
            aT_next = ns.emit(0, self.C_MIX + L * 16)
            for tb in range(NTB):
                aT = aT_next
                groups = [(which, hg) for which in range(2) for hg in range(4)]
                slots = {}

                def load_group(gi):
                    which, hg = groups[gi]
                    slot = self.wslot()
                    sv = slot.t[:].rearrange("p (c f) -> p c f", c=16)
                    c0 = which * NH * HD + hg * 512
                    k.dma("pool", sv, w_in[:, :, c0:c0 + 512], [], slot.rs)
                    slots[gi] = (slot, sv)

                load_group(0)
                load_group(1)
                items = []
                for gi, (which, hg) in enumerate(groups):
                    st = stage[which]
                    for hh in range(4):
                        h = hg * 4 + hh
                        it = dict(nrows=128, gcol=(gq.t[:, which:which + 1], [gq.r]), dst=st.t[:, h, :], dst_rs=[st.rs[h]])
                        if hh == 0 and gi + 2 < len(groups):
                            it['pre'] = (lambda gi=gi: load_group(gi + 2))
                        it['mmf'] = (lambda ps, gi=gi, hh=hh: [(ps.t[:], slots[gi][1][:, kc, hh * 128:(hh + 1) * 128], aT.t[:, kc, :], kc == 0, kc == 15)
                                                             for kc in range(16)])
                        it['readsf'] = (lambda gi=gi: slots[gi][0].rs + aT.rs)
                        if h == NH - 1:
                            if which == 0:
                                it['post'] = (lambda st=st, tb=tb: k.dma("sp", self.qT[:, 0:128, tb * TB:(tb + 1) * TB].rearrange("h p t -> p h t"),
                                                                         st.t[:], st.rs, [self.r_q[tb]]))
                            else:
                                it['post'] = (lambda st=st, tb=tb: k.dma("sp", self.kT[:, :, tb * TB:(tb + 1) * TB].rearrange("h p t -> p h t"),
                                                                         st.t[:], st.rs, [self.r_k[tb]]))
                        items.append(it)
                self.norm_pipeline(items, sqh, rt)
                if tb + 1 < NTB:
                    aT_next = ns.emit(tb + 1, self.C_MIX + L * 16)
                for vc in range(4):
                    slot = self.wslot()
                    sv = slot.t[:].rearrange("p (c f) -> p c f", c=16)
                    c0 = 2 * NH * HD + vc * 512
                    k.dma("pool", sv, w_in[:, :, c0:c0 + 512], [], slot.rs)
                    for ts in range(4):
                        ps = self.bank()
                        k.mm(slot.rs + aT.rs, [ps.r],
                             [(ps.t[:], aT.t[:, kc, ts * 128:(ts + 1) * 128], sv[:, kc, :], kc == 0, kc == 15) for kc in range(16)])
                        if (vc + ts) % 2 == 0:
                            k.op("act", "copy", [ps.r], [vst.rs[ts * 4 + vc]], vst.t[:, ts, vc * 512:(vc + 1) * 512], ps.t[:])
                        else:
                            k.op("dve", "tensor_copy", [ps.r], [vst.rs[ts * 4 + vc]], vst.t[:, ts, vc * 512:(vc + 1) * 512], ps.t[:])
                k.dma("sp", self.V[tb * TB:(tb + 1) * TB, :].rearrange("(s p) f -> p s f", p=128), vst.t[:], vst.rs, [self.r_v[tb]])
                if kind == 'fox':
                    ps = self.bank()
                    k.mm([wf.r] + aT.rs, [ps.r], [(ps.t[0:16, :], wf.t[:, kc, :], aT.t[:, kc, :], kc == 0, kc == 15) for kc in range(16)])
                    k.op("act", "activation", [ps.r, nbf.r], [fe.r], out=fe.t[:], in_=ps.t[0:16, :], func=AF.Exp, scale=-1.0, bias=nbf.t[:, 0:1])
                    k.op("act", "activation", [fe.r], [fe.r], out=fe.t[:], in_=fe.t[:], func=AF.Ln, bias=1.0)
                    if tb == 0:
                        k.op("dve", "tensor_tensor_scan", [fe.r, ones16.r], [cumN.rs[tb]], out=cumN.t[:, 0:TB], data0=ones16.t[:], data1=fe.t[:],
                             initial=0.0, op0=ALU.mult, op1=ALU.add)
                    else:
                        k.op("dve", "tensor_tensor_scan", [fe.r, ones16.r, cumN.rs[tb - 1]], [cumN.rs[tb]], out=cumN.t[:, tb * TB:(tb + 1) * TB],
                             data0=ones16.t[:], data1=fe.t[:], initial=cumN.t[:, tb * TB - 1:tb * TB], op0=ALU.mult, op1=ALU.add)
            k.barrier()
        if kind == 'fox':
            res_ = self.sb(es, "q_res", [16, T], F32)
            cs = [self.sb(es, "q_cs%d" % i, [16, T], BF16) for i in range(3)]
            ncs = [self.sb(es, "q_ncs%d" % i, [16, T], BF16) for i in range(3)]
            onesr = self.sb(es, "q_onesr", [16, T], BF16)
            k.op("dve", "memset", [], [onesr.r], onesr.t[:], 1.0)
            src, src_r = cumN.t, cumN.rs
            for i in range(3):
                k.op("dve", "tensor_copy", src_r, [cs[i].r], cs[i].t[:], src[:])
                k.op("dve", "tensor_scalar", [cs[i].r], [ncs[i].r], ncs[i].t[:], cs[i].t[:], -1.0, None, op0=ALU.mult)
                if i < 2:
                    k.op("dve", "tensor_tensor", src_r + [cs[i].r], [res_.r], res_.t[:], src[:], cs[i].t[:], ALU.subtract)
                    src, src_r = res_.t, [res_.r]
            for i in range(3):
                k.dma("sp", self.augQ[:, i, :], ncs[i].t[:], [ncs[i].r], [self.r_aug[i]])
                k.dma("sp", self.augQ[:, 3 + i, :], onesr.t[:], [onesr.r], [self.r_aug[3 + i]])
                k.dma("sp", self.augK[:, i, :], onesr.t[:], [onesr.r], [self.r_aug[6 + i]])
                k.dma("sp", self.augK[:, 3 + i, :], cs[i].t[:], [cs[i].r], [self.r_aug[9 + i]])

    def phase_mla_proj(self, es, L):
        k, nc = self.k, self.nc
        j = L // 4
        w_in = self.w['mla_w_in'][j].rearrange("(c p) f -> p c f", p=128)
        w_qb = self.w['mla_w_q_b'][j].rearrange("(c p) f -> p c f", p=128)
        w_kvb = self.w['mla_w_kv_b'][j].rearrange("(c p) f -> p c f", p=128)
        SC = 192.0 ** -0.5
        ns = NormStream(self, es, "m_")
        sq = self.sb(es, "m_sq", [128, 4, TB], BF16, n=4)
        tmp = self.sb(es, "m_tmp", [128, TB], F32)
        self.wslots = [self.sb(es, "m_w%d" % i, [128, 8192], BF16, n=2) for i in range(2)]
        self.wslot_i = 0
        sqh = [self.sb(es, "m_sqh%d" % i, [128, TB], BF16) for i in range(3)]
        rt = [self.sb(es, "m_rt%d" % i, [128, TB], F32) for i in range(2)]
        lat = [self.sb(es, "m_lat%d" % i, [128, 4, TB], BF16, n=4) for i in range(2)]
        qst = self.sb(es, "m_qst", [128, NH, TB], BF16, n=NH)
        kst = self.sb(es, "m_kst", [128, NH, TB], BF16, n=NH)
        qrst = self.sb(es, "m_qrst", [64, NH, TB], BF16, n=NH)
        krst = self.sb(es, "m_krst", [64, TB], BF16)
        vst = self.sb(es, "m_vst", [128, 4, NH * HD], BF16, n=16)
        xr = [self.sb(es, "m_xr%d" % i, [64, TB], F32) for i in range(4)]
        t1 = [self.sb(es, "m_t1%d" % i, [64, TB], F32) for i in range(3)]
        g = self.sb(es, "m_g", [128, 16], F32)
        k.dma("sp", g.t[:, 0:4], self.w['mla_q_a_gain'][j].rearrange("(c p) -> p c", p=128), [], [g.r], allow_slow_non_contiguous=True)
        k.dma("sp", g.t[:, 4:8], self.w['mla_kv_a_gain'][j].rearrange("(c p) -> p c", p=128), [], [g.r], allow_slow_non_contiguous=True)
        k.dma("sp", g.t[:, 8:9], self.w['mla_q_gain'][j][0:128].rearrange("(p o) -> p o", o=1), [], [g.r])
        k.dma("sp", g.t[0:64, 9:10], self.w['mla_q_gain'][j][128:192].rearrange("(p o) -> p o", o=1), [], [g.r])
        k.dma("sp", g.t[:, 10:11], self.w['mla_k_gain'][j][0:128].rearrange("(p o) -> p o", o=1), [], [g.r])
        k.dma("sp", g.t[0:64, 11:12], self.w['mla_k_gain'][j][128:192].rearrange("(p o) -> p o", o=1), [], [g.r])
        k.op("dve", "tensor_scalar", [g.r], [g.r], g.t[:, 8:9], g.t[:, 8:9], SC, None, op0=ALU.mult)
        k.op("dve", "tensor_scalar", [g.r], [g.r], g.t[0:64, 9:10], g.t[0:64, 9:10], SC, None, op0=ALU.mult)
        perm = self.sb(es, "m_perm", [64, 64], F32)
        k.dma("sp", perm.t[:], self.c_perm[:, :], [], [perm.r])
        rc = self.sb(es, "m_rc", [64, 2], F32)
        k.dma("sp", rc.t[:], self.c_rope[:, :], [], [rc.r])
        posi = self.sb(es, "m_posi", [64, TB], I32)
        ang = self.sb(es, "m_ang", [64, TB], F32)
        kf = self.sb(es, "m_kf", [64, TB], F32)
        ki = self.sb(es, "m_ki", [64, TB], I32)
        rr = self.sb(es, "m_rr", [64, TB], F32)
        cos2 = self.sb(es, "m_cos", [64, TB], F32)
        sin2 = self.sb(es, "m_sin", [64, TB], F32)
        TWO_PI = 6.283185307179586
        C1 = 6.28125
        C2 = TWO_PI - C1
        PI_S = 3.1415925
        HALF_PI = 1.5707963267948966

        def latent(aT, col0, gcol0, dst):
            slot = self.wslot()
            sv = slot.t[:].rearrange("p (c f) -> p c f", c=16)
            k.dma("pool", sv, w_in[:, :, col0:col0 + 512], [], slot.rs)
            pss = [self.bank() for _ in range(4)]
            for cc in range(4):
                k.mm(slot.rs + aT.rs, [pss[cc].r],
                     [(pss[cc].t[:], sv[:, kc, cc * 128:(cc + 1) * 128], aT.t[:, kc, :], kc == 0, kc == 15) for kc in range(16)])
            ps2 = self.bank()
            for cc in range(4):
                k.op("act", "activation", [pss[cc].r], [sq.rs[cc]], out=sq.t[:, cc, :], in_=pss[cc].t[:], func=AF.Square)
            k.mm([self.ones_bf.r] + sq.rs, [ps2.r], [(ps2.t[:], self.ones_bf.t[:], sq.t[:, cc, :], cc == 0, cc == 3) for cc in range(4)])
            k.op("act", "activation", [ps2.r, self.epsc.r], [tmp.r], out=tmp.t[:], in_=ps2.t[:], func=AF.Ln, scale=1.0 / 512, bias=self.epsc.t[:, 0:1])
            k.op("act", "activation", [tmp.r], [tmp.r], out=tmp.t[:], in_=tmp.t[:], func=AF.Exp, scale=-0.5)
            for cc in range(4):
                k.op("dve", "scalar_tensor_tensor", [pss[cc].r, tmp.r, g.r], [dst.rs[cc]],
                     out=dst.t[:, cc, :], in0=pss[cc].t[:], scalar=g.t[:, gcol0 + cc:gcol0 + cc + 1], in1=tmp.t[:], op0=ALU.mult, op1=ALU.mult)

        aT_next = ns.emit(0, self.C_MIX + L * 16)
        for tb in range(NTB):
            aT = aT_next
            k.dma("sp", posi.t[:], bass.AP(self.pos.tensor, tb * TB, [[0, 64], [1, TB]]), [], [posi.r])
            k.op("dve", "tensor_copy", [posi.r], [ang.r], ang.t[:], posi.t[:])
            k.op("dve", "tensor_scalar", [ang.r, rc.r], [ang.r], ang.t[:], ang.t[:], rc.t[:, 0:1], None, op0=ALU.mult)
            k.op("dve", "tensor_scalar", [ang.r], [kf.r], kf.t[:], ang.t[:], 1.0 / TWO_PI, None, op0=ALU.mult)
            k.op("dve", "tensor_copy", [kf.r], [ki.r], ki.t[:], kf.t[:])
            k.op("dve", "tensor_copy", [ki.r], [kf.r], kf.t[:], ki.t[:])
            k.op("dve", "scalar_tensor_tensor", [kf.r, ang.r], [rr.r], out=rr.t[:], in0=kf.t[:], scalar=-C1, in1=ang.t[:], op0=ALU.mult, op1=ALU.add)
            k.op("dve", "scalar_tensor_tensor", [kf.r, rr.r], [rr.r], out=rr.t[:], in0=kf.t[:], scalar=-C2, in1=rr.t[:], op0=ALU.mult, op1=ALU.add)
            k.op("dve", "tensor_scalar", [rr.r], [rr.r], rr.t[:], rr.t[:], -PI_S, PI_S, op0=ALU.max, op1=ALU.min)
            k.op("act", "activation", [rr.r, rc.r], [sin2.r], out=sin2.t[:], in_=rr.t[:], func=AF.Sin, scale=rc.t[:, 1:2])
            k.op("dve", "tensor_scalar", [rr.r], [kf.r], kf.t[:], rr.t[:], HALF_PI, -TWO_PI, op0=ALU.is_gt, op1=ALU.mult)
            k.op("dve", "scalar_tensor_tensor", [rr.r, kf.r], [rr.r], out=rr.t[:], in0=rr.t[:], scalar=HALF_PI, in1=kf.t[:], op0=ALU.add, op1=ALU.add)
            k.op("dve", "tensor_scalar", [rr.r], [rr.r], rr.t[:], rr.t[:], -PI_S, PI_S, op0=ALU.max, op1=ALU.min)
            k.op("act", "activation", [rr.r], [cos2.r], out=cos2.t[:], in_=rr.t[:], func=AF.Sin)
            latent(aT, 0, 0, lat[0])
            latent(aT, 512, 4, lat[1])

            slots = {}

            def load_slot(name):
                slot = self.wslot()
                if name == 'C':
                    sv = slot.t[:, 0:16 * 64].rearrange("p (c f) -> p c f", c=16)
                    k.dma("pool", sv, w_in[:, :, 1024:1088], [], slot.rs)
                elif name[0] == 'Q':
                    half = int(name[1])
                    sv = slot.t[:, 0:4 * 1536].rearrange("p (c f) -> p c f", c=4)
                    k.dma("pool", sv, w_qb[:, :, half * 1536:(half + 1) * 1536], [], slot.rs)
                else:
                    half = int(name[2])
                    sv = slot.t[:].rearrange("p (c f) -> p c f", c=4)
                    k.dma("pool", sv, w_kvb[:, :, half * 2048:(half + 1) * 2048], [], slot.rs)
                slots[name] = (slot, sv)

            order = ['C', 'Q0', 'Q1', 'KV0', 'KV1']
            load_slot('C')
            load_slot('Q0')
            items = []
            it = dict(nrows=64, gcol=(g.t[0:64, 11:12], [g.r]), rope=True, dst=krst.t[:], dst_rs=[krst.r])
            it['mmf'] = (lambda ps, aT=aT: [(ps.t[0:64, :], slots['C'][1][:, kc, :], aT.t[:, kc, :], kc == 0, kc == 15) for kc in range(16)])
            it['readsf'] = (lambda aT=aT: slots['C'][0].rs + aT.rs)
            it['post'] = (lambda tb=tb: k.dma("sp", self.kropeT[:, tb * TB:(tb + 1) * TB], krst.t[:], [krst.r], [self.r_kr[tb]]))
            items.append(it)
            for half in range(2):
                nm = 'Q%d' % half
                nxt = order[order.index(nm) + 1]
                for part in range(2):
                    for hl in range(8):
                        h = half * 8 + hl
                        if part == 0:
                            it = dict(nrows=128, gcol=(g.t[:, 8:9], [g.r]), dst=qst.t[:, h, :], dst_rs=[qst.rs[h]])
                            it['mmf'] = (lambda ps, nm=nm, hl=hl: [(ps.t[:], slots[nm][1][:, kc, hl * 192:hl * 192 + 128], lat[0].t[:, kc, :], kc == 0, kc == 3)
                                                                  for kc in range(4)])
                            if hl == 0:
                                it['pre'] = (lambda nxt=nxt: load_slot(nxt))
                            if h == NH - 1:
                                it['post'] = (lambda tb=tb: k.dma("sp", self.qT[:, 0:128, tb * TB:(tb + 1) * TB].rearrange("h p t -> p h t"),
                                                                  qst.t[:], qst.rs, [self.r_q[tb]]))
                        else:
                            it = dict(nrows=64, gcol=(g.t[0:64, 9:10], [g.r]), rope=True, dst=qrst.t[:, h, :], dst_rs=[qrst.rs[h]])
                            it['mmf'] = (lambda ps, nm=nm, hl=hl: [(ps.t[0:64, :], slots[nm][1][:, kc, hl * 192 + 128:hl * 192 + 192], lat[0].t[:, kc, :], kc == 0, kc == 3)
                                                                  for kc in range(4)])
                            if h == NH - 1:
                                it['post'] = (lambda tb=tb: k.dma("sp", self.qT[:, 128:192, tb * TB:(tb + 1) * TB].rearrange("h p t -> p h t"),
                                                                  qrst.t[:], qrst.rs, [self.r_q[tb]]))
                        it['readsf'] = (lambda nm=nm: slots[nm][0].rs + lat[0].rs)
                        items.append(it)

            def v_part(half, tb=tb):
                slot, sv = slots['KV%d' % half]
                sv5 = slot.t[:].rearrange("p (c h w f) -> p c h w f", c=4, h=8, w=2)
                for ts in range(4):
                    for hg4 in range(2):
                        ps = self.bank()
                        k.mm(slot.rs + lat[1].rs, [ps.r],
                             [(ps.t[:].rearrange("p (h f) -> p h f", h=4), lat[1].t[:, kc, ts * 128:(ts + 1) * 128],
                               sv5[:, kc, hg4 * 4:(hg4 + 1) * 4, 1, :], kc == 0, kc == 3) for kc in range(4)])
                        vi = half * 2 + hg4
                        if (ts + hg4) % 2 == 0:
                            k.op("act", "copy", [ps.r], [vst.rs[ts * 4 + vi]], vst.t[:, ts, vi * 512:(vi + 1) * 512], ps.t[:])
                        else:
                            k.op("dve", "tensor_copy", [ps.r], [vst.rs[ts * 4 + vi]], vst.t[:, ts, vi * 512:(vi + 1) * 512], ps.t[:])
                if half == 1:
                    k.dma("sp", self.kT[:, :, tb * TB:(tb + 1) * TB].rearrange("h p t -> p h t"), kst.t[:], kst.rs, [self.r_k[tb]])
                    k.dma("sp", self.V[tb * TB:(tb + 1) * TB, :].rearrange("(s p) f -> p s f", p=128), vst.t[:], vst.rs, [self.r_v[tb]])

            for half in range(2):
                nm = 'KV%d' % half
                for hl in range(8):
                    h = half * 8 + hl
                    it = dict(nrows=128, gcol=(g.t[:, 10:11], [g.r]), dst=kst.t[:, h, :], dst_rs=[kst.rs[h]])
                    it['mmf'] = (lambda ps, nm=nm, hl=hl: [(ps.t[:], slots[nm][1][:, kc, hl * 256:hl * 256 + 128], lat[1].t[:, kc, :], kc == 0, kc == 3)
                                                          for kc in range(4)])
                    it['readsf'] = (lambda nm=nm: slots[nm][0].rs + lat[1].rs)
                    if hl == 0 and half == 0:
                        it['pre'] = (lambda: load_slot('KV1'))
                    items.append(it)
            if tb + 1 < NTB:
                aT_next = ns.emit(tb + 1, self.C_MIX + L * 16)
            self.norm_pipeline(items, sqh, rt, xr, t1, rope=(perm, cos2, sin2))
            v_part(0)
            v_part(1)

    def phase_sgu(self, es, L):
        k, nc = self.k, self.nc
        j = L // 4
        w_in = self.w['sgu_w_in'][j].rearrange("(c p) f -> p c f", p=128)
        w_out = self.w['sgu_w_out'][j].rearrange("(c p) d -> p c d", p=128)
        hTv = self.hT.rearrange("(c p) t -> p c t", p=128)
        ns = NormStream(self, es, "s_")
        rsd = Resid(self, es, "s_")
        self.wslots = [self.sb(es, "s_w%d" % i, [128, 8192], BF16, n=2) for i in range(2)]
        self.wslot_i = 0
        uT = self.sb(es, "s_uT", [128, 16, TB], BF16, n=16)
        vg = [self.sb(es, "s_vg%d" % i, [128, D], F32, n=4) for i in range(4)]
        junk = self.sb(es, "s_junk", [128, D], BF16)
        vvn = [self.sb(es, "s_vvn%d" % i, [128, D], BF16) for i in range(4)]
        ss = self.sb(es, "s_ss", [128, 4], F32, n=4)
        wsT = self.sb(es, "s_wsT", [128, 16, 128], BF16, n=16)
        bsb = self.sb(es, "s_bsb", [128, 16, 128], F32)
        vgb = self.sb(es, "s_vgb", [128, D], F32)
        ga = [self.sb(es, "s_ga%d" % i, [128, TB], F32) for i in range(3)]
        gb = [self.sb(es, "s_gb%d" % i, [128, TB], F32) for i in range(3)]
        m1 = [self.sb(es, "s_m1%d" % i, [128, TB], F32) for i in range(2)]
        wl = [self.sb(es, "s_wl%d" % i, [128, 128], F32) for i in range(2)]
        wt = [self.sb(es, "s_wt%d" % i, [128, 128], F32) for i in range(2)]
        for g_ in range(16):
            u = g_ % 2
            k.dma("sp", wl[u].t[:], self.w['sgu_w_s'][j, g_, :, :], [], [wl[u].r])
            ps = self.bank()
            k.mm([wl[u].r, self.ident.r], [ps.r], [(ps.t[:, 0:128], wl[u].t[:], self.ident.t[:])], transpose=True)
            k.op("dve", "tensor_copy", [ps.r], [wt[u].r], wt[u].t[:], ps.t[:, 0:128])
            k.op("pool", "affine_select", [wt[u].r], [wsT.rs[g_]], out=wsT.t[:, g_, :], in_=wt[u].t[:], pattern=[[1, 128]],
                 compare_op=ALU.is_ge, fill=0.0, base=0, channel_multiplier=-1)
        bs = self.w['sgu_b_s'][j]
        k.dma("sp", bsb.t[:].rearrange("p g t -> p (g t)"), bass.AP(bs.tensor, bs.offset, [[0, 128], [1, 16 * 128]]), [], [bsb.r])
        vgn = self.w['sgu_v_gain'][j]
        k.dma("sp", vgb.t[:], bass.AP(vgn.tensor, vgn.offset, [[0, 128], [1, D]]), [], [vgb.r])
        def gelu_pipeline(items):
            banks = self.banks
            for i, it in enumerate(items):
                it['ps'] = banks[i % 6]
                it['ga'] = ga[i % len(ga)]
                it['gb'] = gb[i % len(gb)]

            def g0(it):
                if it.get('pre'):
                    it['pre']()
                k.mm(it['readsf'](), [it['ps'].r], it['mmf'](it['ps']))

            def g1(it):
                k.op("act", "activation", [it['ps'].r], [it['ga'].r], out=it['ga'].t[:], in_=it['ps'].t[:], func=AF.Square)

            def g2(it):
                a_ = it['ga']
                k.op("dve", "tensor_scalar", [a_.r], [a_.r], a_.t[:], a_.t[:], 0.044715, 1.0, op0=ALU.mult, op1=ALU.add)
                k.op("dve", "tensor_tensor", [a_.r, it['ps'].r], [a_.r], a_.t[:], a_.t[:], it['ps'].t[:], ALU.mult)

            def g3(it):
                k.op("act", "activation", [it['ga'].r], [it['gb'].r], out=it['gb'].t[:], in_=it['ga'].t[:], func=AF.Sigmoid, scale=1.5957691216057308)

            def g4(it):
                k.op("dve", "tensor_tensor", [it['gb'].r, it['ps'].r], it['dst_rs'], it['dst'], it['gb'].t[:], it['ps'].t[:], ALU.mult)

            self.pipeline(items, [(0, g0), (1, g1), (2, g2), (3, g3), (4, g4)])

        aT_next = ns.emit(0, self.C_MIX + L * 16)
        for tb in range(NTB):
            aT = aT_next
            slots = {}

            def load_group(gi_):
                slot = self.wslot()
                sv = slot.t[:].rearrange("p (c f) -> p c f", c=16)
                k.dma("pool", sv, w_in[:, :, gi_ * 512:(gi_ + 1) * 512], [], slot.rs)
                slots[gi_] = (slot, sv)

            load_group(0)
            items = []
            for gi_ in range(8):
                for sub in range(4):
                    if gi_ < 4:
                        fch = gi_ * 4 + sub
                        it = dict(dst=uT.t[:, fch, :], dst_rs=[uT.rs[fch]])
                        it['mmf'] = (lambda ps, gi_=gi_, sub=sub, aT=aT: [(ps.t[:], slots[gi_][1][:, kc, sub * 128:(sub + 1) * 128], aT.t[:, kc, :], kc == 0, kc == 15)
                                                                        for kc in range(16)])
                    else:
                        vc, ts = gi_ - 4, sub
                        it = dict(dst=vg[ts].t[:, vc * 512:(vc + 1) * 512], dst_rs=[vg[ts].rs[vc]])
                        it['mmf'] = (lambda ps, gi_=gi_, ts=ts, aT=aT: [(ps.t[:], aT.t[:, kc, ts * 128:(ts + 1) * 128], slots[gi_][1][:, kc, :], kc == 0, kc == 15)
                                                                       for kc in range(16)])
                    it['readsf'] = (lambda gi_=gi_, aT=aT: slots[gi_][0].rs + aT.rs)
                    if sub == 0 and gi_ + 1 < 8:
                        it['pre'] = (lambda gi_=gi_: load_group(gi_ + 1))
                    items.append(it)
            gelu_pipeline(items)
            for ts in range(4):
                k.op("act", "activation", vg[ts].rs, [junk.r], out=junk.t[:], in_=vg[ts].t[:], func=AF.Square)
                k.op("dve", "reduce_sum", [junk.r], [ss.rs[ts]], ss.t[:, ts:ts + 1], junk.t[:], mybir.AxisListType.X)
                k.op("act", "activation", [ss.rs[ts], self.epsc.r], [ss.rs[ts]], out=ss.t[:, ts:ts + 1], in_=ss.t[:, ts:ts + 1], func=AF.Sqrt,
                     scale=1.0 / D, bias=self.epsc.t[:, 0:1])
                k.op("dve", "reciprocal", [ss.rs[ts]], [ss.rs[ts]], ss.t[:, ts:ts + 1], ss.t[:, ts:ts + 1])
                k.op("dve", "scalar_tensor_tensor", vg[ts].rs + [ss.rs[ts], vgb.r], [vvn[ts].r],
                     out=vvn[ts].t[:], in0=vg[ts].t[:], scalar=ss.t[:, ts:ts + 1], in1=vgb.t[:], op0=ALU.mult, op1=ALU.mult)
            for g_ in range(16):
                ps = self.bank()
                k.mm([wsT.rs[g_]] + [vvn[ts].r for ts in range(4)], [ps.r],
                     [(ps.t[:, ts * 128:(ts + 1) * 128], vvn[ts].t[:, g_ * 128:(g_ + 1) * 128], wsT.t[:, g_, :], True, True) for ts in range(4)])
                mm_ = m1[g_ % 2]
                for ts in range(4):
                    k.op("dve", "tensor_tensor", [ps.r, bsb.r], [mm_.r], mm_.t[:, ts * 128:(ts + 1) * 128], ps.t[:, ts * 128:(ts + 1) * 128],
                         bsb.t[:, g_, :], ALU.add)
                k.op("dve", "tensor_tensor", [mm_.r, uT.rs[g_]], [uT.rs[g_]], uT.t[:, g_, :], mm_.t[:], uT.t[:, g_, :], ALU.mult)
            if tb + 1 < NTB:
                aT_next = ns.emit(tb + 1, self.C_MIX + L * 16)
            for dg in range(4):
                slot = self.wslot()
                sv = slot.t[:].rearrange("p (c f) -> p c f", c=16)
                k.dma("pool", sv, w_out[:, :, dg * 512:(dg + 1) * 512], [], slot.rs)
                for dd in range(4):
                    dc = dg * 4 + dd
                    ps = self.bank()
                    rb = rsd.load(tb, dc)
                    k.mm(slot.rs + uT.rs, [ps.r],
                         [(ps.t[:], sv[:, c, dd * 128:(dd + 1) * 128], uT.t[:, c, :], c == 0, c == 15) for c in range(16)])
                    rsd.add_store(rb, tb, dc, ps)

    def phase_attn(self, es, L, kind):
        k, nc = self.k, self.nc
        Vv = self.V.rearrange("(c p) f -> p c f", p=128)
        qh = [[self.sb(es, "a_q%d%d" % (i, c), [128, T], BF16) for c in range(2)] for i in range(2)]
        kh = [[self.sb(es, "a_k%d%d" % (i, c), [128, T], BF16) for c in range(2)] for i in range(2)]
        vh = [[self.sb(es, "a_v%d%d" % (i, c), [128, 16, HD], BF16) for c in range(2)] for i in range(2)]
        if kind == 'fox':
            qa = [[self.sb(es, "a_qa%d%d" % (i, c), [6, T], BF16) for c in range(2)] for i in range(2)]
            ka = [[self.sb(es, "a_ka%d%d" % (i, c), [6, T], BF16) for c in range(2)] for i in range(2)]
        if kind == 'mla':
            qr = [[self.sb(es, "a_qr%d%d" % (i, c), [64, T], BF16) for c in range(2)] for i in range(2)]
            kr = self.sb(es, "a_kr", [64, T], BF16)
            k.dma("sp", kr.t[:], self.kropeT[:, :], self.r_kr, [kr.r])
        ao = [[self.sb(es, "a_ao%d%d" % (i, c), [128, T], BF16, n=NTB) for c in range(2)] for i in range(2)]
        NP = 8
        pT = [self.sb(es, "a_pT%d" % i, [128, TB], BF16) for i in range(NP)]
        acc = [[self.sb(es, "a_acc%d%d" % (c, i), [128, TB], F32) for i in range(2)] for c in range(2)]
        rden = [self.sb(es, "a_rden%d" % c, [128, TB], F32) for c in range(2)]
        ones_f = self.sb(es, "a_ones", [128, 128], F32)
        k.op("dve", "memset", [], [ones_f.r], ones_f.t[:], 1.0)
        banks, tri = self.banks, self.tri

        def load_pair(hp):
            s_ = hp % 2
            for c in range(2):
                h = 2 * hp + c
                k.dma("sp", qh[s_][c].t[:], self.qT[h, 0:128, :], self.r_q, [qh[s_][c].r])
                k.dma("sp", kh[s_][c].t[:], self.kT[h, :, :], self.r_k, [kh[s_][c].r])
                k.dma("sp", vh[s_][c].t[:], Vv[:, :, h * HD:(h + 1) * HD], self.r_v, [vh[s_][c].r])
                if kind == 'fox':
                    k.dma("sp", qa[s_][c].t[:], self.augQ[h, :, :], self.r_aug[0:6], [qa[s_][c].r])
                    k.dma("sp", ka[s_][c].t[:], self.augK[h, :, :], self.r_aug[6:12], [ka[s_][c].r])
                if kind == 'mla':
                    k.dma("sp", qr[s_][c].t[:], self.qT[h, 128:192, :], self.r_q, [qr[s_][c].r])

        tiles = []
        blk = 0
        for hp in range(NH // 2):
            for qb in range(NTB):
                nkc = 4 * qb + 4
                for kc in range(nkc):
                    for c in range(2):
                        jd = kc - 4 * qb
                        tiles.append(dict(hp=hp, s=hp % 2, c=c, qb=qb, kc=kc, jd=jd, c0=(128 * jd if jd > 0 else 0),
                                          first=(kc == 0), last=(kc == nkc - 1), t0=qb * TB, par=blk % 2,
                                          pair_first=(qb == 0 and kc == 0 and c == 0), pair_last=(qb == NTB - 1 and kc == nkc - 1)))
                blk += 1
        pstart = {}
        for i, tl in enumerate(tiles):
            pstart.setdefault(tl['hp'], i)
            tl['prefetch'] = (i == pstart[tl['hp']] + 14)
            tl['u'] = i % NP
            tl['z'] = banks[i % 2]
            tl['O'] = banks[2 + 2 * tl['c'] + tl['par']]
            tl['Dn'] = banks[6 + tl['c']]
            tl['acc'] = acc[tl['c']][tl['par']]

        def s0(tl):
            if tl['pair_first'] and tl['hp'] == 0:
                load_pair(0)
            if tl['prefetch'] and tl['hp'] + 1 < NH // 2:
                load_pair(tl['hp'] + 1)
            s_, c, c0, kc, t0, Z = tl['s'], tl['c'], tl['c0'], tl['kc'], tl['t0'], tl['z']
            q_, k_ = qh[s_][c], kh[s_][c]
            reads = [q_.r, k_.r]
            if kind == 'fox':
                mms = [(Z.t[:, c0:TB], k_.t[:, kc * 128:(kc + 1) * 128], q_.t[:, t0 + c0:t0 + TB], True, False),
                       (Z.t[:, c0:TB], ka[s_][c].t[:, kc * 128:(kc + 1) * 128], qa[s_][c].t[:, t0 + c0:t0 + TB], False, True)]
                reads += [qa[s_][c].r, ka[s_][c].r]
            else:
                mms = [(Z.t[:, c0:TB], k_.t[:, kc * 128:(kc + 1) * 128], q_.t[:, t0 + c0:t0 + TB], True, False),
                       (Z.t[:, c0:TB], kr.t[:, kc * 128:(kc + 1) * 128], qr[s_][c].t[:, t0 + c0:t0 + TB], False, True)]
                reads += [qr[s_][c].r, kr.r]
            k.mm(reads, [Z.r], mms)

        def s1(tl):
            c0, u, Z = tl['c0'], tl['u'], tl['z']
            k.op("act", "activation", [Z.r], [pT[u].r], out=pT[u].t[:, c0:TB], in_=Z.t[:, c0:TB], func=AF.Exp)

        def s2(tl):
            c0, u, a_ = tl['c0'], tl['u'], tl['acc']
            p = pT[u]
            if tl['jd'] >= 0:
                k.op("dve", "tensor_tensor", [p.r, tri.r], [p.r], p.t[:, c0:c0 + 128], p.t[:, c0:c0 + 128], tri.t[:, 0, :], ALU.mult)
            if tl['first']:
                k.op("dve", "tensor_copy", [p.r], [a_.r], a_.t[:], p.t[:])
            else:
                k.op("dve", "tensor_tensor", [p.r, a_.r], [a_.r], a_.t[:, c0:TB], a_.t[:, c0:TB], p.t[:, c0:TB], ALU.add)

        def s3(tl):
            s_, c, c0, u, kc, O = tl['s'], tl['c'], tl['c0'], tl['u'], tl['kc'], tl['O']
            v_ = vh[s_][c]
            k.mm([v_.r, pT[u].r], [O.r], [(O.t[:, c0:TB], v_.t[:, kc, :], pT[u].t[:, c0:TB], tl['first'], tl['last'])])

        def s4(tl):
            if tl['last']:
                Dn, a_ = tl['Dn'], tl['acc']
                k.mm([ones_f.r, a_.r], [Dn.r], [(Dn.t[:], ones_f.t[:], a_.t[:], True, True)])

        def s5(tl):
            if tl['last']:
                Dn, rd = tl['Dn'], rden[tl['c']]
                k.op("act", "activation", [Dn.r], [rd.r], out=rd.t[:], in_=Dn.t[:], func=AF.Ln)
                k.op("act", "activation", [rd.r], [rd.r], out=rd.t[:], in_=rd.t[:], func=AF.Exp, scale=-1.0)

        def s6(tl):
            if tl['last']:
                c, qb, t0, O, rd = tl['c'], tl['qb'], tl['t0'], tl['O'], rden[tl['c']]
                a_ = ao[tl['s']][c]
                k.op("dve", "tensor_tensor", [O.r, rd.r], [a_.rs[qb]], a_.t[:, t0:t0 + TB], O.t[:], rd.t[:], ALU.mult)
                if tl['pair_last']:
                    h = 2 * tl['hp'] + c
                    k.dma("sp", self.aoT[h * HD:(h + 1) * HD, :], a_.t[:], a_.rs, [self.r_ao[h]])

        self.pipeline(tiles, [(0, s0), (1, s1), (2, s2), (3, s3), (4, s4), (5, s5), (6, s6)])

    def pipeline(self, tiles, stages):
        n = len(tiles)
        maxoff = max(off for off, _ in stages)
        order = sorted(stages, key=lambda st: -st[0])
        for step in range(n + maxoff + 1):
            for off, fn in order:
                t = step - off
                if 0 <= t < n:
                    fn(tiles[t])

    def phase_attn_sb(self, es, L):
        k, nc = self.k, self.nc
        Vv = self.V.rearrange("(c p) f -> p c f", p=128)
        qh = [[self.sb(es, "b_q%d%d" % (i, c), [128, T], BF16) for c in range(2)] for i in range(2)]
        kh = [[self.sb(es, "b_k%d%d" % (i, c), [128, T], BF16) for c in range(2)] for i in range(2)]
        vh = [[self.sb(es, "b_v%d%d" % (i, c), [128, 16, HD], BF16) for c in range(2)] for i in range(2)]
        ao = [[self.sb(es, "b_ao%d%d" % (i, c), [128, T], BF16, n=NTB) for c in range(2)] for i in range(2)]
        NP = 12
        e_t = [self.sb(es, "b_e%d" % i, [128, TB], F32) for i in range(NP)]
        ln_t = [self.sb(es, "b_ln%d" % i, [128, TB], F32) for i in range(NP)]
        lf_t = [self.sb(es, "b_lf%d" % i, [128, TB], F32) for i in range(NP)]
        l1_t = [self.sb(es, "b_l1%d" % i, [128, TB], BF16) for i in range(NP)]
        l2_t = [self.sb(es, "b_l2%d" % i, [128, TB], BF16) for i in range(NP)]
        pT = [self.sb(es, "b_p%d" % i, [128, TB], BF16) for i in range(NP)]
        banks, tri, zbf = self.banks, self.tri, self.zbf
        X = [banks[3], banks[4]]
        O = [banks[5], banks[6]]

        def load_pair(hp):
            s_ = hp % 2
            for c in range(2):
                h = 2 * hp + c
                k.dma("sp", qh[s_][c].t[:], self.qT[h, 0:128, :], self.r_q, [qh[s_][c].r])
                k.dma("sp", kh[s_][c].t[:], self.kT[h, :, :], self.r_k, [kh[s_][c].r])
                k.dma("sp", vh[s_][c].t[:], Vv[:, :, h * HD:(h + 1) * HD], self.r_v, [vh[s_][c].r])

        tiles = []
        for hp in range(NH // 2):
            for qb in range(NTB):
                nkc = 4 * qb + 4
                for kc in range(nkc - 1, -1, -1):
                    for c in range(2):
                        jd = kc - 4 * qb
                        tiles.append(dict(hp=hp, s=hp % 2, c=c, qb=qb, kc=kc, jd=jd, c0=(128 * jd if jd > 0 else 0),
                                          first=(kc == nkc - 1), last=(kc == 0), t0=qb * TB,
                                          pair_first=(qb == 0 and kc == nkc - 1 and c == 0), pair_last=(qb == NTB - 1 and kc == 0)))
        pstart = {}
        for i, tl in enumerate(tiles):
            pstart.setdefault(tl['hp'], i)
            tl['prefetch'] = (i == pstart[tl['hp']] + 14)
            tl['u'] = i % NP
            tl['z'] = banks[i % 3]

        def s0(tl):
            if tl['pair_first'] and tl['hp'] == 0:
                load_pair(0)
            if tl['prefetch'] and tl['hp'] + 1 < NH // 2:
                load_pair(tl['hp'] + 1)
            q_, k_ = qh[tl['s']][tl['c']], kh[tl['s']][tl['c']]
            Z, c0, kc, t0 = tl['z'], tl['c0'], tl['kc'], tl['t0']
            k.mm([q_.r, k_.r], [Z.r], [(Z.t[:, c0:TB], k_.t[:, kc * 128:(kc + 1) * 128], q_.t[:, t0 + c0:t0 + TB], True, True)])

        def s1(tl):
            Z, c0, u = tl['z'], tl['c0'], tl['u']
            k.op("act", "activation", [Z.r], [e_t[u].r], out=e_t[u].t[:, c0:TB], in_=Z.t[:, c0:TB], func=AF.Exp, scale=-1.0)
            k.op("act", "activation", [e_t[u].r], [ln_t[u].r], out=ln_t[u].t[:, c0:TB], in_=e_t[u].t[:, c0:TB], func=AF.Ln, bias=1.0)

        def s2(tl):
            Z, c0, u = tl['z'], tl['c0'], tl['u']
            k.op("dve", "tensor_tensor", [Z.r, ln_t[u].r], [lf_t[u].r], lf_t[u].t[:, c0:TB], Z.t[:, c0:TB], ln_t[u].t[:, c0:TB], ALU.add)

        def s3(tl):
            c0, u = tl['c0'], tl['u']
            if tl['jd'] >= 0:
                k.op("pool", "tensor_tensor", [lf_t[u].r, tri.r], [lf_t[u].r], lf_t[u].t[:, c0:c0 + 128], lf_t[u].t[:, c0:c0 + 128],
                     tri.t[:, 3, :], ALU.mult)

        def s3b(tl):
            c0, u = tl['c0'], tl['u']
            k.op("dve", "tensor_copy", [lf_t[u].r], [l1_t[u].r], l1_t[u].t[:, c0:TB], lf_t[u].t[:, c0:TB])

        def s4(tl):
            c0, u = tl['c0'], tl['u']
            k.op("pool", "tensor_tensor", [lf_t[u].r, l1_t[u].r], [l2_t[u].r], l2_t[u].t[:, c0:TB], lf_t[u].t[:, c0:TB], l1_t[u].t[:, c0:TB], ALU.subtract)

        def s5(tl):
            c, c0, u = tl['c'], tl['c0'], tl['u']
            q_ = qh[tl['s']][c]
            if tl['first']:
                k.mm([zbf.r, q_.r], [X[c].r], [(X[c].t[:], zbf.t[:], q_.t[:, 0:TB], True, True)])
            k.mm([tri.r, l1_t[u].r, l2_t[u].r], [X[c].r],
                 [(X[c].t[:, c0:TB], tri.t[:, 1, :], l1_t[u].t[:, c0:TB], False, False, True),
                  (X[c].t[:, c0:TB], tri.t[:, 1, :], l2_t[u].t[:, c0:TB], False, True, True)])

        def s6(tl):
            c, c0, u = tl['c'], tl['c0'], tl['u']
            k.op("dve", "tensor_tensor", [X[c].r, ln_t[u].r], [e_t[u].r], e_t[u].t[:, c0:TB], X[c].t[:, c0:TB], ln_t[u].t[:, c0:TB], ALU.add)

        def s7(tl):
            c, c0, u = tl['c'], tl['c0'], tl['u']
            if not tl['last']:
                k.mm([tri.r, l1_t[u].r, l2_t[u].r], [X[c].r],
                     [(X[c].t[:, c0:TB], tri.t[:, 2, :], l1_t[u].t[:, c0:TB], False, False, True),
                      (X[c].t[:, c0:TB], tri.t[:, 2, :], l2_t[u].t[:, c0:TB], False, True, True)])
            k.op("act", "activation", [e_t[u].r], [pT[u].r], out=pT[u].t[:, c0:TB], in_=e_t[u].t[:, c0:TB], func=AF.Exp, scale=-1.0)

        def s8(tl):
            c0, u = tl['c0'], tl['u']
            if tl['jd'] >= 0:
                k.op("pool", "tensor_tensor", [pT[u].r, tri.r], [pT[u].r], pT[u].t[:, c0:c0 + 128], pT[u].t[:, c0:c0 + 128], tri.t[:, 3, :], ALU.mult)

        def s9(tl):
            c, c0, u, kc = tl['c'], tl['c0'], tl['u'], tl['kc']
            q_, v_ = qh[tl['s']][c], vh[tl['s']][c]
            if tl['first']:
                k.mm([zbf.r, q_.r], [O[c].r], [(O[c].t[:], zbf.t[:], q_.t[:, 0:TB], True, False)])
            k.mm([v_.r, pT[u].r], [O[c].r], [(O[c].t[:, c0:TB], v_.t[:, kc, :], pT[u].t[:, c0:TB], False, tl['last'])])

        def s10(tl):
            if tl['last']:
                c, qb, t0 = tl['c'], tl['qb'], tl['t0']
                a_ = ao[tl['s']][c]
                k.op("act", "copy", [O[c].r], [a_.rs[qb]], a_.t[:, t0:t0 + TB], O[c].t[:])
                if tl['pair_last']:
                    h = 2 * tl['hp'] + c
                    k.dma("sp", self.aoT[h * HD:(h + 1) * HD, :], a_.t[:], a_.rs, [self.r_ao[h]])

        self.pipeline(tiles, [(0, s0), (1, s1), (2, s2), (3, s3), (4, s3b), (5, s4), (6, s5), (7, s6), (8, s7), (9, s8), (10, s9), (11, s10)])

    def phase_oproj(self, es, w_out):
        k, nc = self.k, self.nc
        w = w_out.rearrange("(c p) d -> p c d", p=128)
        aov = self.aoT.rearrange("(c p) t -> p c t", p=128)
        rsd = Resid(self, es, "o_", n=6)
        ab = [self.sb(es, "o_ab%d" % i, [128, 16, TB], BF16) for i in range(2)]
        wres = self.sb(es, "o_wres", [128, 16, D], BF16, n=4)
        for dg in range(4):
            k.dma("pool", wres.t[:, :, dg * 512:(dg + 1) * 512], w[:, :, dg * 512:(dg + 1) * 512], [], [wres.rs[dg]])
        for tb in range(NTB):
            a = ab[tb % 2]
            k.dma("sp", a.t[:], aov[:, :, tb * TB:(tb + 1) * TB], self.r_ao, [a.r])
            for dc in range(16):
                ps = self.bank()
                rb = rsd.load(tb, dc)
                k.mm([wres.rs[dc // 4], a.r], [ps.r],
                     [(ps.t[:], wres.t[:, c, dc * 128:(dc + 1) * 128], a.t[:, c, :], c == 0, c == 15) for c in range(16)])
                rsd.add_store(rb, tb, dc, ps)

def make_consts():
    i = np.arange(128)
    tri = np.stack([
        (i[None, :] >= i[:, None]),
        (i[:, None] > i[None, :]),
        (i[:, None] <= i[None, :]),
        (i[None, :] > i[:, None]),
    ]).astype(np.float32)
    perm = np.zeros((64, 64), np.float32)
    for c in range(64):
        perm[(c + 32) % 64, c] = 1.0
    inv_freq = (np.float32(10000.0) ** (-(np.arange(0, 64, 2, dtype=np.float32)) / np.float32(64))).astype(np.float32)
    rope = np.zeros((64, 2), np.float32)
    rope[:, 0] = np.concatenate([inv_freq, inv_freq])
    rope[:32, 1] = -1.0
    rope[32:, 1] = 1.0
    return {"c_ident": np.eye(128, dtype=np.float32), "c_tri": tri, "c_perm": perm, "c_rope": rope}


def run(inputs, layers=None, n_cores=8, trace=False, debug=False):
    shapes = {n: inputs[n].shape for n in WEIGHT_NAMES}
    P = Prog(layers)
    P.debug = debug
    nc = P.build(shapes)
    consts = make_consts()
    in_maps = []
    for c in range(n_cores):
        m = {"x": np.ascontiguousarray(inputs['x'][c]),
             "positions": np.ascontiguousarray(inputs['positions'][c]).astype(np.int32)}
        for n in WEIGHT_NAMES:
            m[n] = np.ascontiguousarray(inputs[n], dtype=np.float32)
        m.update(consts)
        in_maps.append(m)
    if trace:
        res = run_bass_kernel_spmd(nc, in_maps, core_ids=list(range(n_cores)), trace=True)
    else:
        res = run_bass_kernel_spmd(nc, in_maps, core_ids=list(range(n_cores)))
    return np.stack([r["out"] for r in res.results], axis=0), res, P


def kernel(**inputs):
    inputs = {k_: np.asarray(v) for k_, v in inputs.items()}
    out, _, _ = run(inputs, None, 8)
    return out.astype(np.float32)
```
